# Optimizing a Trainium2 kernel written in Bass

```python
import math
import jax, jax.numpy as jnp
from jax import lax
import numpy as np

D_MODEL = 2048
BATCH = 4
SEQ = 4096
DEPTH = 2

HEAD_DIM = 128
N_HEADS_TOTAL = D_MODEL // HEAD_DIM
N_FOX = N_HEADS_TOTAL // 2
N_DIFF = N_HEADS_TOTAL - N_FOX
FOX_WIDTH = N_FOX * HEAD_DIM
DIFF_WIDTH = N_DIFF * HEAD_DIM
MIX_WIDTH = FOX_WIDTH + DIFF_WIDTH
DIFF_QK_DIM = HEAD_DIM // 2
D_FF = 4 * D_MODEL
ROPE_THETA = 500000.0
ROT_DIM = DIFF_QK_DIM // 4
BLOCK_Q = 128
ALPHA = (2.0 * DEPTH) ** 0.25
BETA = (8.0 * DEPTH) ** -0.25
LN_EPS = 1e-5
RMS_EPS = 1e-5
ADA_SCALE = 0.5

_SIZES = [FOX_WIDTH, FOX_WIDTH, FOX_WIDTH, N_FOX, DIFF_WIDTH, DIFF_WIDTH, DIFF_WIDTH]
IN_COLS = sum(_SIZES)
_SPLITS = [int(v) for v in np.cumsum(_SIZES)[:-1]]

kernel_name = "hybrid_fox_diffattn_deepnorm_adaln"


def _layernorm(x, g, b):
    xf = x.astype(jnp.float32)
    mu = jnp.mean(xf, axis=-1, keepdims=True)
    var = jnp.mean(jnp.square(xf - mu), axis=-1, keepdims=True)
    y = (xf - mu) * lax.rsqrt(var + LN_EPS) * g.astype(jnp.float32) + b.astype(jnp.float32)
    return y.astype(x.dtype)


def _rope(x, cos, sin):
    half = ROT_DIM // 2
    c = cos[:, :, None, None, :]
    s = sin[:, :, None, None, :]
    x1 = x[..., :half]
    x2 = x[..., half:ROT_DIM]
    return jnp.concatenate([x1 * c - x2 * s, x2 * c + x1 * s, x[..., ROT_DIM:]], axis=-1)


def _lambda_init(layer_idx):
    return 0.8 - 0.6 * math.exp(-0.3 * layer_idx)


def _mixer(h, cos, sin, w_in, b_f, lq1, lk1, lq2, lk2, subln_g, w_o, lam_init):
    B, S, _ = h.shape
    f32 = jnp.float32
    proj = (h @ w_in).astype(f32)
    fq, fk, fv, ff, dq, dk, dv = jnp.split(proj, _SPLITS, axis=-1)
    fq = fq.reshape(B, S, N_FOX, HEAD_DIM)
    fk = fk.reshape(B, S, N_FOX, HEAD_DIM)
    fv = fv.reshape(B, S, N_FOX, HEAD_DIM)
    log_f = jax.nn.log_sigmoid(ff + b_f.astype(f32))
    fcum = jnp.cumsum(log_f, axis=1).transpose(0, 2, 1)
    dq = _rope(dq.reshape(B, S, N_DIFF, 2, DIFF_QK_DIM), cos, sin)
    dk = _rope(dk.reshape(B, S, N_DIFF, 2, DIFF_QK_DIM), cos, sin)
    dv = dv.reshape(B, S, N_DIFF, HEAD_DIM)
    lam = (jnp.exp(jnp.sum(lq1.astype(f32) * lk1.astype(f32)))
           - jnp.exp(jnp.sum(lq2.astype(f32) * lk2.astype(f32))) + lam_init)
    fox_scale = HEAD_DIM ** -0.5
    diff_scale = DIFF_QK_DIM ** -0.5
    key_pos = jnp.arange(S)

    def block(i):
        start = i * BLOCK_Q
        q_pos = start + jnp.arange(BLOCK_Q)
        causal = key_pos[None, :] <= q_pos[:, None]
        fq_b = lax.dynamic_slice_in_dim(fq, start, BLOCK_Q, axis=1)
        fc_b = lax.dynamic_slice_in_dim(fcum, start, BLOCK_Q, axis=2)
        s1 = (jnp.einsum('bqhd,bkhd->bhqk', fq_b, fk) * fox_scale
              + fc_b[:, :, :, None] - fcum[:, :, None, :])
        p1 = jax.nn.softmax(jnp.where(causal, s1, -jnp.inf), axis=-1)
        fox_o = jnp.einsum('bhqk,bkhd->bqhd', p1, fv)
        dq_b = lax.dynamic_slice_in_dim(dq, start, BLOCK_Q, axis=1)
        s2 = jnp.einsum('bqhcd,bkhcd->bhcqk', dq_b, dk) * diff_scale
        p2 = jax.nn.softmax(jnp.where(causal, s2, -jnp.inf), axis=-1)
        a = p2[:, :, 0] - lam * p2[:, :, 1]
        diff_o = jnp.einsum('bhqk,bkhd->bqhd', a, dv)
        return fox_o, diff_o

    fox_o, diff_o = lax.map(block, jnp.arange(S // BLOCK_Q))
    fox_o = jnp.moveaxis(fox_o, 0, 1).reshape(B, S, FOX_WIDTH)
    diff_o = jnp.moveaxis(diff_o, 0, 1).reshape(B, S, N_DIFF, HEAD_DIM)
    diff_o = diff_o * lax.rsqrt(jnp.mean(jnp.square(diff_o), axis=-1, keepdims=True) + RMS_EPS)
    diff_o = (diff_o * subln_g.astype(f32) * (1.0 - lam_init)).reshape(B, S, DIFF_WIDTH)
    mixed = jnp.concatenate([fox_o, diff_o], axis=-1).astype(h.dtype)
    return mixed @ w_o


def _mlp(h, w_up, w_down):
    return jnp.square(jax.nn.relu(h @ w_up)) @ w_down


def setup_inputs(seed: int = 0) -> dict:
    key = jax.random.key(seed)
    ks = jax.random.split(key, 20)
    f32 = jnp.float32
    x = jax.random.normal(ks[0], (BATCH, SEQ, D_MODEL), f32)
    c = jax.random.normal(ks[1], (BATCH, D_MODEL), f32)
    positions = jnp.broadcast_to(jnp.arange(SEQ, dtype=jnp.int32)[None, :], (BATCH, SEQ))
    w_ada = jax.random.normal(ks[2], (DEPTH, D_MODEL, 6 * D_MODEL), f32) * (ADA_SCALE * D_MODEL ** -0.5)
    b_ada = jax.random.normal(ks[3], (DEPTH, 6 * D_MODEL), f32) * 0.02
    col_scale = jnp.concatenate([
        jnp.ones((2 * FOX_WIDTH,), f32), jnp.full((FOX_WIDTH,), BETA, f32),
        jnp.ones((N_FOX + 2 * DIFF_WIDTH,), f32), jnp.full((DIFF_WIDTH,), BETA, f32)])
    w_in = jax.random.normal(ks[4], (DEPTH, D_MODEL, IN_COLS), f32) * (D_MODEL ** -0.5) * col_scale
    b_f = 2.0 + 0.5 * jax.random.normal(ks[5], (DEPTH, N_FOX), f32)
    lambda_q1 = jax.random.normal(ks[6], (DEPTH, DIFF_QK_DIM), f32) * 0.1
    lambda_k1 = jax.random.normal(ks[7], (DEPTH, DIFF_QK_DIM), f32) * 0.1
    lambda_q2 = jax.random.normal(ks[8], (DEPTH, DIFF_QK_DIM), f32) * 0.1
    lambda_k2 = jax.random.normal(ks[9], (DEPTH, DIFF_QK_DIM), f32) * 0.1
    subln_g = 1.0 + 0.02 * jax.random.normal(ks[10], (DEPTH, HEAD_DIM), f32)
    w_o = jax.random.normal(ks[11], (DEPTH, MIX_WIDTH, D_MODEL), f32) * (MIX_WIDTH ** -0.5) * BETA
    ln1_g = 1.0 + 0.02 * jax.random.normal(ks[12], (DEPTH, D_MODEL), f32)
    ln1_b = 0.02 * jax.random.normal(ks[13], (DEPTH, D_MODEL), f32)
    w_up = jax.random.normal(ks[14], (DEPTH, D_MODEL, D_FF), f32) * (D_MODEL ** -0.5)
    w_down = jax.random.normal(ks[15], (DEPTH, D_FF, D_MODEL), f32) * (D_FF ** -0.5) * BETA
    ln2_g = 1.0 + 0.02 * jax.random.normal(ks[16], (DEPTH, D_MODEL), f32)
    ln2_b = 0.02 * jax.random.normal(ks[17], (DEPTH, D_MODEL), f32)
    return {"x": x, "c": c, "positions": positions, "w_ada": w_ada, "b_ada": b_ada,
            "w_in": w_in, "b_f": b_f, "lambda_q1": lambda_q1, "lambda_k1": lambda_k1,
            "lambda_q2": lambda_q2, "lambda_k2": lambda_k2, "subln_g": subln_g,
            "w_o": w_o, "ln1_g": ln1_g, "ln1_b": ln1_b, "w_up": w_up, "w_down": w_down,
            "ln2_g": ln2_g, "ln2_b": ln2_b}


def reference(x, c, positions, w_ada, b_ada, w_in, b_f, lambda_q1, lambda_k1,
              lambda_q2, lambda_k2, subln_g, w_o, ln1_g, ln1_b, w_up, w_down,
              ln2_g, ln2_b):
    inv_freq = ROPE_THETA ** (-jnp.arange(0, ROT_DIM, 2, dtype=jnp.float32) / ROT_DIM)
    ang = positions.astype(jnp.float32)[:, :, None] * inv_freq
    cos, sin = jnp.cos(ang), jnp.sin(ang)
    c_act = jax.nn.silu(c)
    for l in range(DEPTH):
        mod = c_act @ w_ada[l] + b_ada[l]
        sh_a, sc_a, g_a, sh_m, sc_m, g_m = jnp.split(mod[:, None, :], 6, axis=-1)
        h = x * (1.0 + sc_a) + sh_a
        y = _mixer(h, cos, sin, w_in[l], b_f[l], lambda_q1[l], lambda_k1[l],
                   lambda_q2[l], lambda_k2[l], subln_g[l], w_o[l], _lambda_init(l))
        x = _layernorm(ALPHA * x + (1.0 + g_a) * y, ln1_g[l], ln1_b[l])
        h = x * (1.0 + sc_m) + sh_m
        y = _mlp(h, w_up[l], w_down[l])
        x = _layernorm(ALPHA * x + (1.0 + g_m) * y, ln2_g[l], ln2_b[l])
    return x
```

```python
import contextlib
import math
import numpy as np
import ml_dtypes
import concourse.bass as bass
import concourse.mybir as mybir
from concourse.bass_utils import run_bass_kernel_spmd

F32 = mybir.dt.float32
BF16 = mybir.dt.bfloat16
I32 = mybir.dt.int32
AF = mybir.ActivationFunctionType
ALU = mybir.AluOpType

D = 2048
SEQ = 4096
NB = 4
DFF = 8192
ALPHA = (2.0 * 2) ** 0.25
LN_EPS = 1e-5
RMS_EPS = 1e-5
FOX_SCALE = 128 ** -0.5
DIFF_SCALE = 64 ** -0.5
NCOL = 32 * 128 + 4
TOK = 2048
TWO_PI = 2.0 * math.pi

ENGS = ("sp", "act", "dve", "pool", "pe")
NDMASEM = 8


def lam_init(l):
    return 0.8 - 0.6 * math.exp(-0.3 * l)


class Sched:
    _uid = 0

    def __init__(self, nc):
        self.nc = nc
        self.ops = []
        self.last_w = {}
        self.readers = {}

    def add(self, eng, fn, reads=(), writes=(), dma=False):
        i = len(self.ops)
        deps = set()
        for r in reads:
            if r in self.last_w:
                deps.add(self.last_w[r])
        for w in writes:
            if w in self.last_w:
                deps.add(self.last_w[w])
            deps.update(self.readers.get(w, ()))
        for r in reads:
            self.readers.setdefault(r, []).append(i)
        for w in writes:
            self.last_w[w] = i
            self.readers[w] = []
        deps.discard(i)
        self.ops.append(dict(eng=eng, fn=fn, deps=deps, dma=dma))
        return i

    def dma(self, q, out, in_, reads=(), writes=()):
        return self.add(q, lambda e: e.dma_start(out=out, in_=in_), reads, writes, dma=True)

    def emit(self):
        nc = self.nc
        ops = self.ops
        for o in ops:
            if o["eng"] == "pe":
                o["deps"] = {d for d in o["deps"] if ops[d]["eng"] != "pe" or ops[d]["dma"]}
        needed = set()
        for o in ops:
            needed.update(o["deps"])
        cnt = {e: 0 for e in ENGS}
        dcnt = {e: 0 for e in ENGS}
        sems = {}
        allsems = []

        def newsem(name):
            h = nc.alloc_semaphore(name=name)
            allsems.append(h)
            return h

        if True:
            for e in ENGS:
                Sched._uid += 1
                sems[e] = newsem("s%d_c_%s" % (Sched._uid, e))
                sems["d_" + e] = [newsem("s%d_d_%s%d" % (Sched._uid, e, k)) for k in range(NDMASEM)]
            for i, o in enumerate(ops):
                e = o["eng"]
                if o["dma"]:
                    k = dcnt[e]
                    dcnt[e] += 1
                    sem = sems["d_" + e][k % NDMASEM]
                    o["sig"] = (sem, 16 * (k // NDMASEM + 1))
                    o["pre"] = (sem, 16 * (k // NDMASEM)) if k >= NDMASEM else None
                    o["do_sig"] = True
                else:
                    o["pre"] = None
                    o["do_sig"] = i in needed
                    if o["do_sig"]:
                        cnt[e] += 1
                        o["sig"] = (sems[e], cnt[e])
            per_eng = {e: [i for i, o in enumerate(ops) if o["eng"] == e] for e in ENGS}
            engobj = {"sp": "sync", "act": "scalar", "dve": "vector", "pool": "gpsimd", "pe": "tensor"}
            with nc.Block() as block:
                def make(ename):
                    def body(eng):
                        waited = {}

                        def wait(sem, val):
                            if val <= 0 or waited.get(id(sem), 0) >= val:
                                return
                            waited[id(sem)] = val
                            eng.wait_ge(sem, val)

                        for i in per_eng[ename]:
                            o = ops[i]
                            if o["pre"] is not None:
                                wait(*o["pre"])
                            for d in sorted(o["deps"]):
                                wait(*ops[d]["sig"])
                            ins = o["fn"](eng)
                            if o["do_sig"]:
                                ins.then_inc(o["sig"][0], 16 if o["dma"] else 1)
                        if ename == "sp":
                            for q in ENGS:
                                n = dcnt[q]
                                for k in range(min(n, NDMASEM)):
                                    uses = (n - 1 - k) // NDMASEM + 1
                                    eng.wait_ge(sems["d_" + q][k], 16 * uses)
                    return body

                for ename in ENGS:
                    getattr(block, engobj[ename])(make(ename))
            nc.clear_and_free_semaphores(allsems)
            nc.all_engine_barrier()


class Prog:
    def __init__(self, kinds, fused=False):
        self.nc = bass.Bass("TRN2", target_bir_lowering=False)
        self.kinds = kinds
        self.dram = {}
        self.fused = fused
        self.regs = {}

    def core_vals(self, e):
        if "v" not in self.regs:
            pid = e.partition_id()
            r = e.snap(pid % 2)
            b = e.snap((pid - r) // 2)
            self.regs["v"] = (r, b)
        return self.regs["v"]

    def D(self, name, shape, dtype):
        if name not in self.dram:
            kind = self.kinds.get(name, "Internal")
            self.dram[name] = self.nc.dram_tensor(name, list(shape), dtype, kind=kind).ap()
        return self.dram[name]


class Phase:
    def __init__(self, prog, tag):
        self.prog = prog
        self.nc = prog.nc
        self.tag = tag
        self.st = contextlib.ExitStack()
        self.S = Sched(self.nc)

    def __enter__(self):
        self.st.__enter__()
        self.banks = [self.st.enter_context(self.nc.psum_tensor("%s_bank%d" % (self.tag, i), [128, 512], F32))
                      for i in range(8)]
        return self

    def __exit__(self, *a):
        if a[0] is None:
            self.S.emit()
        return self.st.__exit__(*a)

    def sb(self, name, shape, dtype):
        return self.st.enter_context(self.nc.sbuf_tensor("%s_%s" % (self.tag, name), list(shape), dtype))

    def ident(self):
        S = self.S
        ones = self.sb("c_ones", [128, 128], F32)
        ident = self.sb("c_ident", [128, 128], F32)
        S.add("pool", lambda e: e.memset(ones[:], 1.0), writes=["c_ones"])
        S.add("pool", lambda e: e.affine_select(out=ident[:], in_=ones[:], pattern=[[-1, 128]], base=0,
                                               channel_multiplier=1, compare_op=ALU.is_equal, fill=0.0),
              reads=["c_ones"], writes=["ident"])
        return ident


class Rot:
    def __init__(self, ph, name, n, shape, dtype):
        self.tiles = [ph.sb("%s%d" % (name, i), shape, dtype) for i in range(n)]
        self.names = ["%s%d" % (name, i) for i in range(n)]
        self.i = 0

    def next(self):
        k = self.i % len(self.tiles)
        self.i += 1
        return self.tiles[k], self.names[k]


def ph_mod(prog, fused=False):
    c = prog.D("c", [NB, D], F32)
    w = prog.D("w_ada_s", [2, D, 1536], F32)
    b = prog.D("b_ada_s", [2, 1536], F32)
    modp = prog.D("modp", [2, NB, 1536], F32)
    with Phase(prog, "mod") as ph:
        S = ph.S
        ident = ph.ident()
        cin = ph.sb("cin", [64, 128], F32)
        cT = ph.sb("cT", [128, 64], F32)
        S.dma("sp", cin[:], c.rearrange("b (j p) -> (b j) p", p=128), writes=["cin"])
        bk = ph.banks[0]
        S.add("pe", lambda e: e.transpose(bk[:, 0:64], cin[:], ident[0:64, 0:64]),
              reads=["cin", "ident"], writes=["bank0"])
        S.add("act", lambda e: e.activation(out=cT[:], in_=bk[:, 0:64], func=AF.Silu),
              reads=["bank0"], writes=["cT"])
        wt = Rot(ph, "wt", 2, [128, 16, 512], F32)
        bt = Rot(ph, "bt", 2, [NB, 512], F32)
        ot = Rot(ph, "ot", 2, [NB, 512], F32)
        k = 0
        for l in range(2):
            for cg in range(3):
                wtile, wn = wt.next()
                btile, bn = bt.next()
                otile, on = ot.next()
                cs = slice(cg * 512, (cg + 1) * 512)
                S.dma("sp", wtile[:], w[l, :, cs].rearrange("(j p) c -> p j c", p=128), writes=[wn])
                S.dma("sp", btile[:], b[l, cs].partition_broadcast(NB), writes=[bn])
                bank = ph.banks[1 + k % 2]
                bname = "bank%d" % (1 + k % 2)
                for j in range(16):
                    S.add("pe", lambda e, j=j, bank=bank, wtile=wtile: e.matmul(
                        bank[0:NB, :], cT[:, j:64:16], wtile[:, j, :], start=(j == 0), stop=(j == 15)),
                        reads=["cT", wn], writes=[bname])
                S.add("dve", lambda e, bank=bank, otile=otile, btile=btile: e.tensor_tensor(
                    out=otile[:], in0=bank[0:NB, :], in1=btile[:], op=ALU.add),
                    reads=[bname, bn], writes=[on])
                S.dma("sp", modp[l, :, cs], otile[:], reads=[on], writes=["modp"])
                k += 1


def _mm(S, out, lhsT, rhs, start, stop, reads, writes):
    S.add("pe", lambda e: e.matmul(out, lhsT, rhs, start=start, stop=stop), reads, writes)


def _tr(S, out, in_, ident, reads, writes):
    S.add("pe", lambda e: e.transpose(out, in_, ident), reads, writes)


def _act(S, out, in_, func, reads, writes, scale=None, bias=None):
    kw = {}
    if scale is not None:
        kw["scale"] = scale
    if bias is not None:
        kw["bias"] = bias
    S.add("act", lambda e: e.activation(out=out, in_=in_, func=func, **kw), reads, writes)


def _tt(S, eng, out, in0, in1, op, reads, writes):
    S.add(eng, lambda e: e.tensor_tensor(out=out, in0=in0, in1=in1, op=op), reads, writes)


def _ts(S, eng, out, in0, s1, op0, reads, writes, s2=None, op1=None):
    if op1 is None:
        S.add(eng, lambda e: e.tensor_scalar(out=out, in0=in0, scalar1=s1, scalar2=None, op0=op0), reads, writes)
    else:
        S.add(eng, lambda e: e.tensor_scalar(out=out, in0=in0, scalar1=s1, scalar2=s2, op0=op0, op1=op1),
              reads, writes)


def _stt(S, out, in0, scalar, in1, op0, op1, reads, writes):
    S.add("dve", lambda e: e.scalar_tensor_tensor(out=out, in0=in0, scalar=scalar, in1=in1, op0=op0, op1=op1),
          reads, writes)


def _copy(S, eng, out, in_, reads, writes):
    S.add(eng, lambda e: e.tensor_copy(out, in_), reads, writes)


def _memset(S, eng, ap, val, writes):
    S.add(eng, lambda e: e.memset(ap, val), (), writes)


def load_modT(ph, ident, modb, l):
    S = ph.S
    rows = ph.sb("modrows", [64, 128], F32)
    modT = ph.sb("modT", [128, 64], F32)
    S.dma("sp", rows[0:32, :], modb[l, 0:2 * D].rearrange("(r p) -> r p", p=128), writes=["modrows"])
    S.dma("sp", rows[32:64, :], modb[l, 3 * D:5 * D].rearrange("(r p) -> r p", p=128), writes=["modrows"])
    bk = ph.banks[7]
    _tr(S, bk[:, 0:64], rows[:], ident[0:64, 0:64], ["modrows", "ident"], ["bank7"])
    _copy(S, "dve", modT[:], bk[:, 0:64], ["bank7"], ["modT"])
    _ts(S, "dve", modT[:, 16:32], modT[:, 16:32], 1.0, ALU.add, ["modT"], ["modT"])
    _ts(S, "dve", modT[:, 48:64], modT[:, 48:64], 1.0, ALU.add, ["modT"], ["modT"])
    return modT


def emit_hT(ph, ident, xt, xn, modT, sh_off, sc_off, hst, hsn, col0, banks):
    S = ph.S
    for jg in range(4):
        bi = banks[jg % len(banks)]
        bank, bn = ph.banks[bi], "bank%d" % bi
        for q in range(4):
            j = jg * 4 + q
            _tr(S, bank[:, q * 128:(q + 1) * 128], xt[:, j * 128:(j + 1) * 128], ident[:], [xn, "ident"], [bn])
        for q in range(4):
            j = jg * 4 + q
            _act(S, hst[:, j, col0:col0 + 128], bank[:, q * 128:(q + 1) * 128], AF.Identity,
                 [bn, "modT"], [hsn + ".%d" % j],
                 scale=modT[:, sc_off + j:sc_off + j + 1], bias=modT[:, sh_off + j:sh_off + j + 1])


def ph_pre(prog):
    x = prog.D("x_own", [TOK, D], F32)
    modb = prog.D("modb", [2, 6 * D], F32)
    hTn = prog.D("hTn0", [16, 128, TOK], BF16)
    with Phase(prog, "pre") as ph:
        S = ph.S
        ident = ph.ident()
        modT = load_modT(ph, ident, modb, 0)
        xp = Rot(ph, "x", 3, [128, D], F32)
        hp = Rot(ph, "hst", 2, [128, 16, 512], BF16)
        for g in range(4):
            hst, hsn = hp.next()
            for q in range(4):
                ts_ = g * 4 + q
                xt, xn = xp.next()
                S.dma("sp", xt[:], x[ts_ * 128:(ts_ + 1) * 128, :], writes=[xn])
                emit_hT(ph, ident, xt, xn, modT, 0, 16, hst, hsn, q * 128, [0, 1, 2, 3])
            S.dma("sp", hTn[:, :, g * 512:(g + 1) * 512].rearrange("j p t -> p j t"), hst[:],
                  reads=[hsn + ".%d" % j for j in range(16)], writes=["hTn"])


MAGIC = 12582912.0


def ph_rope(prog):
    pos = prog.D("pos_b", [SEQ], I32)
    rconst = prog.D("rconst", [128, 2], F32)
    Ctab = prog.D("Ctab", [128, SEQ], F32)
    Stab = prog.D("Stab", [128, SEQ], F32)
    with Phase(prog, "rope") as ph:
        S = ph.S
        rc = ph.sb("rc", [128, 2], F32)
        S.dma("sp", rc[:], rconst, writes=["rc"])
        W = 1024
        pp = Rot(ph, "posi", 2, [128, W], I32)
        ap_ = Rot(ph, "ang", 2, [128, W], F32)
        yp = Rot(ph, "y", 2, [128, W], F32)
        rp = Rot(ph, "r", 2, [128, W], F32)
        op_ = Rot(ph, "o", 2, [128, W], F32)
        for ck in range(SEQ // W):
            posi, pn = pp.next()
            ang, an = ap_.next()
            S.dma("sp", posi[:], pos[ck * W:(ck + 1) * W].partition_broadcast(128), writes=[pn])
            _copy(S, "dve", ang[:], posi[:], [pn], [an])
            _ts(S, "dve", ang[:], ang[:], rc[:, 0:1], ALU.mult, [an, "rc"], [an])
            for which in range(2):
                y, yn = yp.next()
                r, rn = rp.next()
                o, on = op_.next()
                if which == 1:
                    _ts(S, "dve", r[:], ang[:], math.pi / 2, ALU.add, [an], [rn])
                    src, sn = r, rn
                else:
                    src, sn = ang, an
                _ts(S, "dve", y[:], src[:], 1.0 / TWO_PI, ALU.mult, [sn], [yn])
                _ts(S, "dve", y[:], y[:], MAGIC, ALU.add, [yn], [yn])
                _ts(S, "dve", y[:], y[:], MAGIC, ALU.subtract, [yn], [yn])
                _stt(S, r[:], y[:], -TWO_PI, src[:], ALU.mult, ALU.add, [yn, sn], [rn])
                _ts(S, "dve", r[:], r[:], -math.pi, ALU.max, [rn], [rn], s2=math.pi, op1=ALU.min)
                _act(S, o[:], r[:], AF.Sin, [rn], [on])
                if which == 0:
                    _ts(S, "dve", o[:], o[:], rc[:, 1:2], ALU.mult, [on, "rc"], [on], s2=DIFF_SCALE ** 0.5, op1=ALU.mult)
                else:
                    _ts(S, "dve", o[:], o[:], DIFF_SCALE ** 0.5, ALU.mult, [on], [on])
                S.dma("sp", (Stab if which == 0 else Ctab)[:, ck * W:(ck + 1) * W], o[:], reads=[on],
                      writes=["tab%d" % which])


def ph_a1(prog, l):
    hTfull = prog.D("hTfull%d" % l, [4, 2, 4, 128, TOK], BF16)
    wc = prog.D("w_core", [2, D, NCOL], F32)
    bfs = prog.D("b_f_s", [2, 4], F32)
    Ctab = prog.D("Ctab", [128, SEQ], F32)
    Stab = prog.D("Stab", [128, SEQ], F32)
    QT = prog.D("QT", [8, 128, SEQ], BF16)
    KT = prog.D("KT", [8, 128, SEQ], BF16)
    V = prog.D("V", [SEQ, 1024], BF16)
    QA = prog.D("QA", [4, 6, SEQ], BF16)
    KA = prog.D("KA", [4, 6, SEQ], BF16)
    with Phase(prog, "a1_%d" % l) as ph:
        S = ph.S
        hT = ph.sb("hT", [128, 16, SEQ], BF16)
        for rr in range(2):
            for jg in range(4):
                S.dma("sp", hT[:, jg * 4:(jg + 1) * 4, rr * TOK:(rr + 1) * TOK],
                      hTfull[jg, rr].rearrange("j p t -> p j t"), writes=["hT.%d.%d" % (rr, jg)])

        def hres(tok0, j):
            return "hT.%d.%d" % (tok0 // TOK, j // 4)

        nb = [0]

        def nextbank():
            b = nb[0] % 6
            nb[0] += 1
            return ph.banks[b], "bank%d" % b

        wp = Rot(ph, "wp", 4, [128, 16, 128], BF16)
        ost = Rot(ph, "ost", 2, [128, 2048], BF16)

        def load_wpiece(ci):
            t, n = wp.next()
            S.dma("pool", t[:], wc[l, :, ci * 128:(ci + 1) * 128].rearrange("(j p) c -> p j c", p=128), writes=[n])
            return t, n

        for ci in range(8):
            hl, isq = ci % 4, ci < 4
            wt_, wn = load_wpiece(ci)
            for half in range(2):
                ot, on = ost.next()
                for t4 in range(4):
                    tt = half * 4 + t4
                    bank, bn = nextbank()
                    for j in range(16):
                        _mm(S, bank[:], wt_[:, j, :], hT[:, j, tt * 512:(tt + 1) * 512], j == 0, j == 15,
                            [wn, hres(tt * 512, j)], [bn])
                    _act(S, ot[:, t4 * 512:(t4 + 1) * 512], bank[:], AF.Copy, [bn], [on + ".%d" % t4],
                         scale=(FOX_SCALE if isq else 1.0))
                S.dma("sp", (QT if isq else KT)[hl, :, half * 2048:(half + 1) * 2048], ot[:],
                      reads=[on + ".%d" % t4 for t4 in range(4)], writes=["QKd"])
        ctp = Rot(ph, "ct", 2, [128, 512], F32)
        stp = Rot(ph, "st", 2, [128, 512], F32)
        t1p = Rot(ph, "t1", 1, [128, 512], F32)
        t2p = Rot(ph, "t2", 1, [128, 512], F32)
        for ci in range(8, 16):
            hl, isq = 4 + ci % 4, ci < 12
            wm, wmn = load_wpiece(ci)
            wr, wrn = load_wpiece(ci + 8)
            for half in range(2):
                ot, on = ost.next()
                for t4 in range(4):
                    tt = half * 4 + t4
                    ct, cn = ctp.next()
                    st_, sn = stp.next()
                    S.dma("sp", ct[:], Ctab[:, tt * 512:(tt + 1) * 512], writes=[cn])
                    S.dma("sp", st_[:], Stab[:, tt * 512:(tt + 1) * 512], writes=[sn])
                    bm, bmn = nextbank()
                    br, brn = nextbank()
                    for j in range(16):
                        _mm(S, bm[:], wm[:, j, :], hT[:, j, tt * 512:(tt + 1) * 512], j == 0, j == 15,
                            [wmn, hres(tt * 512, j)], [bmn])
                    for j in range(16):
                        _mm(S, br[:], wr[:, j, :], hT[:, j, tt * 512:(tt + 1) * 512], j == 0, j == 15,
                            [wrn, hres(tt * 512, j)], [brn])
                    t1, t1n = t1p.next()
                    t2, t2n = t2p.next()
                    _tt(S, "dve", t1[:], bm[:], ct[:], ALU.mult, [bmn, cn], [t1n])
                    _tt(S, "dve", t2[:], br[:], st_[:], ALU.mult, [brn, sn], [t2n])
                    _tt(S, "pool", ot[:, t4 * 512:(t4 + 1) * 512], t1[:], t2[:], ALU.add, [t1n, t2n],
                        [on + ".%d" % t4])
                S.dma("sp", (QT if isq else KT)[hl, :, half * 2048:(half + 1) * 2048], ot[:],
                      reads=[on + ".%d" % t4 for t4 in range(4)], writes=["QKd"])
        wv = ph.sb("wv", [128, 16, 512], BF16)
        vsp = Rot(ph, "vs", 2, [128, 512], BF16)
        for g in range(2):
            S.dma("pool", wv[:], wc[l, :, 3072 + g * 512:3072 + (g + 1) * 512].rearrange("(j p) c -> p j c", p=128),
                  writes=["wv"])
            for tk in range(32):
                bank, bn = nextbank()
                for j in range(16):
                    _mm(S, bank[:], hT[:, j, tk * 128:(tk + 1) * 128], wv[:, j, :], j == 0, j == 15,
                        ["wv", hres(tk * 128, j)], [bn])
                vs, vn = vsp.next()
                _act(S, vs[:], bank[:], AF.Copy, [bn], [vn])
                S.dma("sp", V[tk * 128:(tk + 1) * 128, g * 512:(g + 1) * 512], vs[:], reads=[vn], writes=["Vd"])
        wg = ph.sb("wg", [128, 16, 4], BF16)
        S.dma("pool", wg[:], wc[l, :, 4096:4100].rearrange("(j p) c -> p j c", p=128), writes=["wg"])
        bfc = ph.sb("bfc", [4, 1], F32)
        S.dma("sp", bfc[:], bfs[l].rearrange("(p o) -> p o", o=1), writes=["bfc"])
        _ts(S, "dve", bfc[:], bfc[:], -1.0, ALU.mult, ["bfc"], ["bfc"])
        one4 = ph.sb("one4", [4, 512], F32)
        _memset(S, "dve", one4[:], 1.0, ["one4"])
        oneb = ph.sb("oneb", [4, 3, 512], BF16)
        _memset(S, "dve", oneb[:], 1.0, ["oneb"])
        ep = Rot(ph, "ge", 1, [4, 512], F32)
        gp = Rot(ph, "gG", 2, [4, 512], F32)
        r1p = Rot(ph, "gr1", 1, [4, 512], F32)
        pcp = Rot(ph, "gpc", 1, [4, 3, 512], BF16)
        ngp = Rot(ph, "gng", 1, [4, 3, 512], BF16)
        prevG = None
        for tt in range(8):
            bank, bn = nextbank()
            for j in range(16):
                _mm(S, bank[0:4, :], wg[:, j, :], hT[:, j, tt * 512:(tt + 1) * 512], j == 0, j == 15,
                    ["wg", hres(tt * 512, j)], [bn])
            e_, en = ep.next()
            _act(S, e_[:], bank[0:4, :], AF.Exp, [bn, "bfc"], [en], scale=-1.0, bias=bfc[:, 0:1])
            _act(S, e_[:], e_[:], AF.Ln, [en], [en], bias=1.0)
            G, gn = gp.next()
            if prevG is None:
                S.add("dve", lambda e, G=G, e_=e_: e.tensor_tensor_scan(
                    out=G[:], data0=one4[:], data1=e_[:], initial=0.0, op0=ALU.mult, op1=ALU.add),
                    ["one4", en], [gn])
            else:
                pG, pgn = prevG
                S.add("dve", lambda e, G=G, e_=e_, pG=pG: e.tensor_tensor_scan(
                    out=G[:], data0=one4[:], data1=e_[:], initial=pG[:, 511:512], op0=ALU.mult, op1=ALU.add),
                    ["one4", en, pgn], [gn])
            prevG = (G, gn)
            pc, pcn = pcp.next()
            ng, ngn = ngp.next()
            r1, r1n = r1p.next()
            _copy(S, "dve", pc[:, 0, :], G[:], [gn], [pcn])
            _tt(S, "dve", r1[:], G[:], pc[:, 0, :], ALU.subtract, [gn, pcn], [r1n])
            _copy(S, "dve", pc[:, 1, :], r1[:], [r1n], [pcn])
            _tt(S, "dve", r1[:], r1[:], pc[:, 1, :], ALU.subtract, [r1n, pcn], [r1n])
            _copy(S, "dve", pc[:, 2, :], r1[:], [r1n], [pcn])
            _ts(S, "dve", ng[:], pc[:], -1.0, ALU.mult, [pcn], [ngn])
            cs = slice(tt * 512, (tt + 1) * 512)
            S.dma("sp", KA[:, 3:6, cs], pc[:], reads=[pcn], writes=["KAd"])
            S.dma("sp", QA[:, 0:3, cs], ng[:], reads=[ngn], writes=["QAd"])
            S.dma("sp", KA[:, 0:3, cs], oneb[:], reads=["oneb"], writes=["KAd"])
            S.dma("sp", QA[:, 3:6, cs], oneb[:], reads=["oneb"], writes=["QAd"])


def ph_a2(prog, l):
    QT = prog.D("QT", [8, 128, SEQ], BF16)
    KT = prog.D("KT", [8, 128, SEQ], BF16)
    V = prog.D("V", [SEQ, 1024], BF16)
    QA = prog.D("QA", [4, 6, SEQ], BF16)
    KA = prog.D("KA", [4, 6, SEQ], BF16)
    lams = [prog.D(n, [2, 64], F32) for n in ("lambda_q1", "lambda_k1", "lambda_q2", "lambda_k2")]
    subg = prog.D("subln_g", [2, 128], F32)
    MT = prog.D("MTh", [2, 1024, TOK], BF16)
    with Phase(prog, "a2_%d" % l) as ph:
        S = ph.S
        onesb = ph.sb("onesb", [128, 128], BF16)
        tri = ph.sb("tri", [128, 128], F32)
        zf = ph.sb("zf", [128, 128], F32)
        onesf = ph.sb("onesf", [128, 128], F32)
        _memset(S, "pool", onesb[:], 1.0, ["onesb"])
        _memset(S, "pool", onesf[:], 1.0 / 128.0, ["onesf"])
        _memset(S, "pool", zf[:], 0.0, ["zf"])
        S.add("pool", lambda e: e.affine_select(out=tri[:], in_=zf[:], pattern=[[1, 128]], base=0,
                                               channel_multiplier=-1, compare_op=ALU.is_ge, fill=-30000.0),
              ["zf"], ["tri"])
        lt = [ph.sb("lam%d" % i, [128, 64], F32) for i in range(4)]
        for i in range(4):
            S.dma("sp", lt[i][:], lams[i][l].partition_broadcast(128), writes=["lam%d" % i])
        lsum = ph.sb("lsum", [128, 2], F32)
        nlam = ph.sb("nlam", [128, 1], F32)
        for m in range(2):
            _tt(S, "dve", lt[2 * m][:], lt[2 * m][:], lt[2 * m + 1][:], ALU.mult,
                ["lam%d" % (2 * m), "lam%d" % (2 * m + 1)], ["lam%d" % (2 * m)])
            S.add("dve", lambda e, m=m: e.reduce_sum(out=lsum[:, m:m + 1], in_=lt[2 * m][:],
                                                     axis=mybir.AxisListType.X),
                  ["lam%d" % (2 * m)], ["lsum"])
        _act(S, lsum[:], lsum[:], AF.Exp, ["lsum"], ["lsum"])
        _stt(S, nlam[:], lsum[:, 1:2], -lam_init(l), lsum[:, 0:1], ALU.add, ALU.subtract, ["lsum"], ["nlam"])
        gcol = ph.sb("gcol", [128, 1], F32)
        S.dma("sp", gcol[:], subg[l].rearrange("(p o) -> p o", o=1), writes=["gcol"])
        _ts(S, "dve", gcol[:], gcol[:], 1.0 - lam_init(l), ALU.mult, ["gcol"], ["gcol"])

        NSET = 2
        KTt = Rot(ph, "KTt", NSET, [128, SEQ], BF16)
        QTt = Rot(ph, "QTt", NSET, [128, SEQ], BF16)
        Vt = Rot(ph, "Vt", NSET, [128, 32, 128], BF16)
        QAt = Rot(ph, "QAt", NSET, [128, SEQ], BF16)
        KAt = Rot(ph, "KAt", NSET, [128, SEQ], BF16)
        for i in range(NSET):
            _memset(S, "pool", QAt.tiles[i][:], 0.0, [QAt.names[i]])
            _memset(S, "pool", KAt.tiles[i][:], 0.0, [KAt.names[i]])
        mtp = Rot(ph, "mt", 2, [128, SEQ], BF16)
        pTp = Rot(ph, "pT", 4, [128, 512], BF16)
        rvp = Rot(ph, "rinv", 2, [128, 512], F32)
        o1p = Rot(ph, "o1", 1, [128, 512], F32)
        o2p = Rot(ph, "o2", 1, [128, 512], F32)
        dp = Rot(ph, "dd", 1, [128, 512], F32)
        sqp = Rot(ph, "sq", 1, [128, 512], F32)
        rsp = Rot(ph, "rstd", 1, [128, 512], F32)
        ucount = [0]

        for hl in range(8):
            fox = hl < 4
            kt, ktn = KTt.next()
            qt, qtn = QTt.next()
            vt, vtn = Vt.next()
            S.dma("sp", kt[:], KT[hl], writes=[ktn])
            S.dma("sp", qt[:], QT[hl], writes=[qtn])
            for kq in range(4):
                S.dma("sp", vt[:, kq * 8:(kq + 1) * 8, :],
                      V[kq * 1024:(kq + 1) * 1024, hl * 128:(hl + 1) * 128].rearrange("(kb p) d -> p kb d", p=128),
                      writes=[vtn + ".%d" % kq])
            if fox:
                qa, qan = QAt.next()
                ka, kan = KAt.next()
                S.dma("sp", qa[0:6, :], QA[hl], writes=[qan])
                S.dma("sp", ka[0:6, :], KA[hl], writes=[kan])
            mt, mtn = mtp.next()
            maps = [(0, 128)] if fox else [(0, 64), (64, 128)]
            units = []
            for qb in range(8):
                for kb in range(4 * qb + 4):
                    for m in range(len(maps)):
                        units.append((qb, kb, m))
            info = {}

            def emit_S(u):
                qb, kb, m = units[u]
                g = ucount[0] + u
                bi = g % 3
                bank, bn = ph.banks[bi], "bank%d" % bi
                r_ = kb - 4 * qb
                c0 = 128 * r_ if r_ > 0 else 0
                lo, hi = maps[m]
                q0 = qb * 512
                _mm(S, bank[:, c0:512], kt[lo:hi, kb * 128:(kb + 1) * 128], qt[lo:hi, q0 + c0:q0 + 512],
                    True, not fox, [ktn, qtn], [bn])
                if fox:
                    _mm(S, bank[:, c0:512], ka[:, kb * 128:(kb + 1) * 128], qa[:, q0 + c0:q0 + 512],
                        False, True, [kan, qan], [bn])
                info[u] = (bank, bn, c0, r_)

            def emit_rest(u):
                qb, kb, m = units[u]
                bank, bn, c0, r_ = info.pop(u)
                pT, pn = pTp.next()
                if r_ >= 0:
                    _tt(S, "dve", bank[:, c0:c0 + 128], bank[:, c0:c0 + 128], tri[:], ALU.add, [bn, "tri"], [bn])
                _act(S, pT[:, c0:512], bank[:, c0:512], AF.Exp, [bn], [pn])
                pair = (qb % 2) if fox else m
                bo, bon = ph.banks[4 + 2 * pair], "bank%d" % (4 + 2 * pair)
                brr, brn = ph.banks[5 + 2 * pair], "bank%d" % (5 + 2 * pair)
                last = kb == 4 * qb + 3
                _mm(S, bo[:, c0:512], vt[:, kb, :], pT[:, c0:512], kb == 0, last, [vtn + ".%d" % (kb // 8), pn], [bon])
                _mm(S, brr[:, c0:512], onesb[:], pT[:, c0:512], kb == 0, last, ["onesb", pn], [brn])
                if last and m == len(maps) - 1:
                    cs = slice(qb * 512, (qb + 1) * 512)
                    if fox:
                        rv, rvn = rvp.next()
                        S.add("dve", lambda e, rv=rv, brr=brr: e.reciprocal(rv[:], brr[:]), [brn], [rvn])
                        _tt(S, "dve", mt[:, cs], bo[:], rv[:], ALU.mult, [bon, rvn], [mtn + ".%d" % qb])
                    else:
                        outs = []
                        for mm_, (op_, nm) in enumerate(((o1p, "o1"), (o2p, "o2"))):
                            bo_, bon_ = ph.banks[4 + 2 * mm_], "bank%d" % (4 + 2 * mm_)
                            br_, brn_ = ph.banks[5 + 2 * mm_], "bank%d" % (5 + 2 * mm_)
                            rv, rvn = rvp.next()
                            S.add("dve", lambda e, rv=rv, br_=br_: e.reciprocal(rv[:], br_[:]), [brn_], [rvn])
                            o_, on_ = op_.next()
                            _tt(S, "dve", o_[:], bo_[:], rv[:], ALU.mult, [bon_, rvn], [on_])
                            outs.append((o_, on_))
                        d_, dn = dp.next()
                        _stt(S, d_[:], outs[1][0][:], nlam[:, 0:1], outs[0][0][:], ALU.mult, ALU.add,
                             [outs[0][1], outs[1][1], "nlam"], [dn])
                        sq, sqn = sqp.next()
                        _tt(S, "pool", sq[:], d_[:], d_[:], ALU.mult, [dn], [sqn])
                        _mm(S, ph.banks[3][:], onesf[:], sq[:], True, True, ["onesf", sqn], ["bank3"])
                        rs, rsn = rsp.next()
                        _act(S, rs[:], ph.banks[3][:], AF.Ln, ["bank3"], [rsn], bias=RMS_EPS)
                        _act(S, rs[:], rs[:], AF.Exp, [rsn], [rsn], scale=-0.5)
                        _stt(S, mt[:, cs], d_[:], gcol[:, 0:1], rs[:], ALU.mult, ALU.mult, [dn, rsn, "gcol"],
                             [mtn + ".%d" % qb])

            n = len(units)
            emit_S(0)
            if n > 1:
                emit_S(1)
            for i in range(n):
                if i + 2 < n:
                    emit_S(i + 2)
                emit_rest(i)
            ucount[0] += n
            for hf in range(2):
                S.dma("sp", MT[hf, hl * 128:(hl + 1) * 128, :], mt[:, hf * TOK:(hf + 1) * TOK],
                      reads=[mtn + ".%d" % qb for qb in range(4 * hf, 4 * hf + 4)], writes=["MTd"])


class LNBufs:
    def __init__(self, ph):
        self.stats = ph.sb("ln_stats", [128, 24], F32)
        self.mv = ph.sb("ln_mv", [128, 2], F32)
        self.rstd = ph.sb("ln_rstd", [128, 1], F32)
        self.nmr = ph.sb("ln_nmr", [128, 1], F32)


def emit_ln(ph, lb, u, un, lng, lnb, xo, xon):
    S = ph.S
    for c in range(4):
        S.add("dve", lambda e, c=c: e.bn_stats(out=lb.stats[:, c * 6:(c + 1) * 6], in_=u[:, c * 512:(c + 1) * 512]),
              [un], ["ln_stats"])
    S.add("dve", lambda e: e.bn_aggr(out=lb.mv[:], in_=lb.stats[:]), ["ln_stats"], ["ln_mv"])
    _act(S, lb.rstd[:], lb.mv[:, 1:2], AF.Ln, ["ln_mv"], ["ln_rstd"], bias=LN_EPS)
    _act(S, lb.rstd[:], lb.rstd[:], AF.Exp, ["ln_rstd"], ["ln_rstd"], scale=-0.5)
    _stt(S, lb.nmr[:], lb.mv[:, 0:1], -1.0, lb.rstd[:], ALU.mult, ALU.mult, ["ln_mv", "ln_rstd"], ["ln_nmr"])
    _act(S, xo[:], u[:], AF.Identity, [un, "ln_rstd", "ln_nmr"], [xon], scale=lb.rstd[:, 0:1], bias=lb.nmr[:, 0:1])
    _tt(S, "pool", xo[:], xo[:], lng[:], ALU.mult, [xon, "lng"], [xon])
    _tt(S, "pool", xo[:], xo[:], lnb[:], ALU.add, [xon, "lnb"], [xon])


def load_bcast(ph, name, src1d, n=D):
    t = ph.sb(name, [128, n], F32)
    ph.S.dma("sp", t[:], src1d.partition_broadcast(128), writes=[name])
    return t


def ph_b3(prog, l):
    MTx = prog.D("MTx%d" % l, [2, 1024, TOK], BF16)
    xres = prog.D("x_own", [TOK, D], F32) if l == 0 else prog.D("x2", [TOK, D], F32)
    wo = prog.D("w_o", [2, D, D], F32)
    modb = prog.D("modb", [2, 6 * D], F32)
    g1 = prog.D("ln1_g", [2, D], F32)
    b1 = prog.D("ln1_b", [2, D], F32)
    x1 = prog.D("x1", [TOK, D], F32)
    h2T = prog.D("h2T", [16, 128, TOK], BF16)
    with Phase(prog, "b3_%d" % l) as ph:
        S = ph.S
        ident = ph.ident()
        modT = load_modT(ph, ident, modb, l)
        G1 = load_bcast(ph, "G1", modb[l, 2 * D:3 * D])
        _ts(S, "pool", G1[:], G1[:], 1.0, ALU.add, ["G1"], ["G1"])
        lng = load_bcast(ph, "lng", g1[l])
        lnb = load_bcast(ph, "lnb", b1[l])
        wot = ph.sb("wo", [128, 16, D], BF16)
        for g4 in range(4):
            S.dma("pool", wot[:, g4 * 4:(g4 + 1) * 4, :],
                  wo[l, g4 * 512:(g4 + 1) * 512, :].rearrange("(g p) c -> p g c", p=128), writes=["wo.%d" % g4])
        lb = LNBufs(ph)
        mtp = Rot(ph, "mtt", 2, [128, 16, 512], BF16)
        xp = Rot(ph, "xt", 2, [128, D], F32)
        up = Rot(ph, "u", 1, [128, D], F32)
        xop = Rot(ph, "xo", 2, [128, D], F32)
        hp = Rot(ph, "hst", 1, [128, 16, 512], BF16)
        for g in range(4):
            mtt, mtn = mtp.next()
            tsl = slice(g * 512, (g + 1) * 512)
            for j in range(2):
                src = MTx[j].rearrange("(h p) t -> p h t", p=128)
                S.dma("sp", mtt[:, 4 * j:4 * j + 4, :], src[:, 0:4, tsl], writes=[mtn + ".%d" % j])
                S.dma("sp", mtt[:, 8 + 4 * j:12 + 4 * j, :], src[:, 4:8, tsl], writes=[mtn + ".%d" % (2 + j)])
            mres = [mtn + ".%d" % i for i in range(4)]
            hst, hsn = hp.next()
            for q in range(4):
                ts_ = g * 4 + q
                xt, xn = xp.next()
                S.dma("sp", xt[:], xres[ts_ * 128:(ts_ + 1) * 128, :], writes=[xn])
                u, un = up.next()
                for cg in range(4):
                    bank, bn = ph.banks[cg], "bank%d" % cg
                    for gg in range(16):
                        _mm(S, bank[:], mtt[:, gg, q * 128:(q + 1) * 128], wot[:, gg, cg * 512:(cg + 1) * 512],
                            gg == 0, gg == 15, mres + ["wo.%d" % (gg // 4)], [bn])
                    _tt(S, "dve", u[:, cg * 512:(cg + 1) * 512], bank[:], G1[:, cg * 512:(cg + 1) * 512], ALU.mult,
                        [bn, "G1"], [un])
                _stt(S, u[:], xt[:], ALPHA, u[:], ALU.mult, ALU.add, [xn, un], [un])
                xo, xon = xop.next()
                emit_ln(ph, lb, u, un, lng, lnb, xo, xon)
                S.dma("sp", x1[ts_ * 128:(ts_ + 1) * 128, :], xo[:], reads=[xon], writes=["x1d"])
                emit_hT(ph, ident, xo, xon, modT, 32, 48, hst, hsn, q * 128, [4, 5, 6])
            S.dma("sp", h2T[:, :, tsl].rearrange("j p t -> p j t"), hst[:],
                  reads=[hsn + ".%d" % j for j in range(16)], writes=["h2Td"])


def ph_b4(prog, l):
    h2T = prog.D("h2T", [16, 128, TOK], BF16)
    wu = prog.D("w_up", [2, D, DFF], F32)
    wd = prog.D("w_down", [2, DFF, D], F32)
    y2 = prog.D("y2", [TOK, D], F32)
    T = 1024
    with Phase(prog, "b4_%d" % l) as ph:
        S = ph.S
        h2t = ph.sb("h2t", [128, 16, T], BF16)
        aT = ph.sb("aT", [128, 64, T], BF16)
        wup = Rot(ph, "wu", 3, [128, 16, 128], BF16)
        wdp = Rot(ph, "wd", 3, [128, 4, 512], BF16)
        rp = Rot(ph, "relu", 2, [128, 512], F32)
        yp = Rot(ph, "ys", 4, [128, 512], F32)
        nbk = [0]
        for tile in range(TOK // T):
            t0 = tile * T
            for jg in range(4):
                S.dma("sp", h2t[:, jg * 4:(jg + 1) * 4, :],
                      h2T[jg * 4:(jg + 1) * 4, :, t0:t0 + T].rearrange("j p t -> p j t"), writes=["h2t.%d" % jg])
            for fc in range(64):
                w_, wn = wup.next()
                S.dma("pool", w_[:], wu[l, :, fc * 128:(fc + 1) * 128].rearrange("(j p) c -> p j c", p=128),
                      writes=[wn])
                for th in range(2):
                    bi = nbk[0] % 8
                    nbk[0] += 1
                    bank, bn = ph.banks[bi], "bank%d" % bi
                    for j in range(16):
                        _mm(S, bank[:], w_[:, j, :], h2t[:, j, th * 512:(th + 1) * 512], j == 0, j == 15,
                            [wn, "h2t.%d" % (j // 4)], [bn])
                    r_, rn = rp.next()
                    _act(S, r_[:], bank[:], AF.Relu, [bn], [rn])
                    _tt(S, "pool", aT[:, fc, th * 512:(th + 1) * 512], r_[:], r_[:], ALU.mult, [rn],
                        ["aT.%d.%d" % (fc, th)])
            for cg in range(4):
                for fg in range(16):
                    w_, wn = wdp.next()
                    S.dma("pool", w_[:], wd[l, fg * 512:(fg + 1) * 512, cg * 512:(cg + 1) * 512].rearrange(
                        "(f p) c -> p f c", p=128), writes=[wn])
                    for f in range(4):
                        fc = fg * 4 + f
                        for ts_ in range(8):
                            _mm(S, ph.banks[ts_][:], aT[:, fc, ts_ * 128:(ts_ + 1) * 128], w_[:, f, :],
                                fc == 0, fc == 63, [wn, "aT.%d.%d" % (fc, ts_ // 4)], ["bank%d" % ts_])
                for ts_ in range(8):
                    ys, yn = yp.next()
                    if ts_ % 2 == 0:
                        _copy(S, "dve", ys[:], ph.banks[ts_][:], ["bank%d" % ts_], [yn])
                    else:
                        _act(S, ys[:], ph.banks[ts_][:], AF.Copy, ["bank%d" % ts_], [yn])
                    S.dma("sp", y2[t0 + ts_ * 128:t0 + (ts_ + 1) * 128, cg * 512:(cg + 1) * 512], ys[:], reads=[yn],
                          writes=["y2d"])


def ph_b5(prog, l):
    y2 = prog.D("y2", [TOK, D], F32)
    x1 = prog.D("x1", [TOK, D], F32)
    modb = prog.D("modb", [2, 6 * D], F32)
    g2 = prog.D("ln2_g", [2, D], F32)
    b2 = prog.D("ln2_b", [2, D], F32)
    last = l == 1
    xout = prog.D("out", [TOK, D], F32) if last else prog.D("x2", [TOK, D], F32)
    hTn = None if last else prog.D("hTn1", [16, 128, TOK], BF16)
    with Phase(prog, "b5_%d" % l) as ph:
        S = ph.S
        ident = ph.ident()
        modT = None if last else load_modT(ph, ident, modb, l + 1)
        G2 = load_bcast(ph, "G2", modb[l, 5 * D:6 * D])
        _ts(S, "pool", G2[:], G2[:], 1.0, ALU.add, ["G2"], ["G2"])
        lng = load_bcast(ph, "lng", g2[l])
        lnb = load_bcast(ph, "lnb", b2[l])
        lb = LNBufs(ph)
        yp = Rot(ph, "yt", 2, [128, D], F32)
        xp = Rot(ph, "xt", 2, [128, D], F32)
        xop = Rot(ph, "xo", 2, [128, D], F32)
        hp = Rot(ph, "hst", 2, [128, 16, 512], BF16)
        for g in range(4):
            if not last:
                hst, hsn = hp.next()
            for q in range(4):
                ts_ = g * 4 + q
                rows = slice(ts_ * 128, (ts_ + 1) * 128)
                yt, yn = yp.next()
                xt, xn = xp.next()
                S.dma("sp", yt[:], y2[rows, :], writes=[yn])
                S.dma("sp", xt[:], x1[rows, :], writes=[xn])
                _tt(S, "dve", yt[:], yt[:], G2[:], ALU.mult, [yn, "G2"], [yn])
                _stt(S, yt[:], xt[:], ALPHA, yt[:], ALU.mult, ALU.add, [xn, yn], [yn])
                xo, xon = xop.next()
                emit_ln(ph, lb, yt, yn, lng, lnb, xo, xon)
                S.dma("sp", xout[rows, :], xo[:], reads=[xon], writes=["xoutd"])
                if not last:
                    emit_hT(ph, ident, xo, xon, modT, 0, 16, hst, hsn, q * 128, [0, 1, 2, 3])
            if not last:
                S.dma("sp", hTn[:, :, g * 512:(g + 1) * 512].rearrange("j p t -> p j t"), hst[:],
                      reads=[hsn + ".%d" % j for j in range(16)], writes=["hTnd"])


def _core_cols(r):
    off = {"fq": 0, "fk": 1024, "fv": 2048, "ff": 3072, "dq": 3080, "dk": 4104, "dv": 5128}
    heads = [4 * r + i for i in range(4)]
    cols = []
    for name in ("fq", "fk", "dq", "dk"):
        for h in heads:
            cols += [off[name] + h * 128 + d for d in range(128)]
    for name in ("dq", "dk"):
        for h in heads:
            for d in range(128):
                i = d % 64
                src = d + 8 if i < 8 else (d - 8 if i < 16 else d)
                cols.append(off[name] + h * 128 + src)
    for name in ("fv", "dv"):
        for h in heads:
            cols += [off[name] + h * 128 + d for d in range(128)]
    cols += [off["ff"] + h for h in heads]
    return np.array(cols)


def _rconst():
    rc = np.zeros((128, 2), np.float32)
    inv = (np.float32(500000.0) ** (-np.arange(0, 16, 2, dtype=np.float32) / np.float32(16))).astype(np.float32)
    for base in (0, 64):
        for i in range(16):
            rc[base + i, 0] = inv[i % 8]
            rc[base + i, 1] = -1.0 if i < 8 else 1.0
    return rc


def _launch(phases, ins, outs, in_maps):
    kinds = {k: "ExternalInput" for k in ins}
    kinds.update({k: "ExternalOutput" for k in outs})
    prog = Prog(kinds)
    for p in phases:
        p(prog)
    used = [k for k in ins if k in prog.dram]
    maps = [{k: np.ascontiguousarray(m[k]) for k in used} for m in in_maps]
    res = run_bass_kernel_spmd(prog.nc, maps, core_ids=list(range(8)))
    return res.results


def kernel_unfused(x, c, positions, w_ada, b_ada, w_in, b_f, lambda_q1, lambda_k1, lambda_q2, lambda_k2, subln_g,
           w_o, ln1_g, ln1_b, w_up, w_down, ln2_g, ln2_b):
    f32 = np.float32
    x = np.asarray(x, f32)
    w_cores = [np.ascontiguousarray(np.asarray(w_in, f32)[:, :, _core_cols(r)]) for r in range(2)]
    rc = _rconst()
    base = []
    for cid in range(8):
        b, r = cid // 2, cid % 2
        base.append({
            "c": np.asarray(c, f32),
            "w_ada_s": np.asarray(w_ada, f32)[:, :, cid * 1536:(cid + 1) * 1536],
            "b_ada_s": np.asarray(b_ada, f32)[:, cid * 1536:(cid + 1) * 1536],
            "x_own": x[b, r * TOK:(r + 1) * TOK],
            "pos_b": np.asarray(positions, np.int32)[b],
            "rconst": rc,
            "w_core": w_cores[r],
            "b_f_s": np.asarray(b_f, f32)[:, 4 * r:4 * r + 4],
            "lambda_q1": np.asarray(lambda_q1, f32), "lambda_k1": np.asarray(lambda_k1, f32),
            "lambda_q2": np.asarray(lambda_q2, f32), "lambda_k2": np.asarray(lambda_k2, f32),
            "subln_g": np.asarray(subln_g, f32), "w_o": np.asarray(w_o, f32),
            "ln1_g": np.asarray(ln1_g, f32), "ln1_b": np.asarray(ln1_b, f32),
            "w_up": np.asarray(w_up, f32), "w_down": np.asarray(w_down, f32),
            "ln2_g": np.asarray(ln2_g, f32), "ln2_b": np.asarray(ln2_b, f32),
        })
    innames = list(base[0].keys())
    r0 = _launch([ph_mod], innames, ["modp"], base)
    mod_all = np.concatenate([r["modp"] for r in r0], axis=2)
    for cid in range(8):
        base[cid]["modb"] = mod_all[:, cid // 2, :]
    r1 = _launch([ph_pre], innames + ["modb"], ["hTn0"], base)
    hT = [r["hTn0"] for r in r1]
    out = None
    for l in range(2):
        for cid in range(8):
            p = cid - cid % 2
            base[cid]["hTfull%d" % l] = np.stack(
                [np.stack([hT[p + rr][jg * 4:(jg + 1) * 4] for rr in range(2)]) for jg in range(4)])
        ra = _launch([ph_rope, lambda pr, l=l: ph_a1(pr, l), lambda pr, l=l: ph_a2(pr, l)],
                     innames + ["modb", "hTfull%d" % l], ["MTh"], base)
        for cid in range(8):
            p, r = cid - cid % 2, cid % 2
            base[cid]["MTx%d" % l] = np.stack([ra[p]["MTh"][r], ra[p + 1]["MTh"][r]])
        extra = ["modb", "MTx%d" % l] + (["x2"] if l == 1 else [])
        outs = ["out"] if l == 1 else ["x2", "hTn1"]
        rb = _launch([lambda pr, l=l: ph_b3(pr, l), lambda pr, l=l: ph_b4(pr, l), lambda pr, l=l: ph_b5(pr, l)],
                     innames + extra, outs, base)
        if l == 0:
            for cid in range(8):
                base[cid]["x2"] = rb[cid]["x2"]
            hT = [r["hTn1"] for r in rb]
        else:
            out = np.stack([np.concatenate([rb[2 * b]["out"], rb[2 * b + 1]["out"]], axis=0) for b in range(4)])
    return out.astype(np.float32)


PAIRS = [[0, 1], [2, 3], [4, 5], [6, 7]]
_cc_n = [0]


def ph_cc(prog, kind, groups, src2d, dst2d):
    nc = prog.nc
    _cc_n[0] += 1
    cc = nc.alloc_semaphore(name="cc_sem%d" % _cc_n[0])
    with nc.Block() as block:
        @block.gpsimd
        def _(g):
            g.collective_compute(kind, ALU.bypass, replica_groups=groups, ins=[src2d.opt()],
                                 outs=[dst2d.opt()]).then_inc(cc)
            g.wait_ge(cc, 1)
    nc.clear_and_free_semaphores([cc])
    nc.all_engine_barrier()


def ph_modcopy(prog):
    nc = prog.nc
    modag = prog.D("modag", [8, 2, NB, 1536], F32)
    modb = prog.D("modb", [2, 6 * D], F32)
    _cc_n[0] += 1
    sm = nc.alloc_semaphore(name="mc_sem%d" % _cc_n[0])
    with nc.Block() as block:
        @block.sync
        def _(e):
            _, bidx = prog.core_vals(e)
            src = modag.rearrange("k l b i -> b k l i")[bass.ts(bidx, 1)][0]
            e.dma_start(out=modb.rearrange("l (k i) -> k l i", i=1536), in_=src).then_inc(sm, 16)
            e.wait_ge(sm, 16)
    nc.clear_and_free_semaphores([sm])
    nc.all_engine_barrier()


def ph_mtsel(prog, l):
    nc = prog.nc
    MTag = prog.D("MTag%d" % l, [4, 2, 512, TOK], BF16)
    MTx = prog.D("MTx%d" % l, [2, 1024, TOK], BF16)
    _cc_n[0] += 1
    sm = nc.alloc_semaphore(name="ms_sem%d" % _cc_n[0])
    with nc.Block() as block:
        @block.sync
        def _(e):
            r, _ = prog.core_vals(e)
            src = MTag.rearrange("(h mh) s m t -> h mh s m t", mh=2)[bass.ts(r, 1)][0]
            for mh in range(2):
                e.dma_start(out=MTx[:, mh * 512:(mh + 1) * 512, :], in_=src[mh]).then_inc(sm, 16)
            e.wait_ge(sm, 32)
    nc.clear_and_free_semaphores([sm])
    nc.all_engine_barrier()


def build_fused():
    ins = ["c", "w_ada_s", "b_ada_s", "x_own", "pos_b", "rconst", "w_core", "b_f_s", "lambda_q1", "lambda_k1",
           "lambda_q2", "lambda_k2", "subln_g", "w_o", "ln1_g", "ln1_b", "w_up", "w_down", "ln2_g", "ln2_b"]
    kinds = {k: "ExternalInput" for k in ins}
    kinds["out"] = "ExternalOutput"
    prog = Prog(kinds, fused=True)
    ph_mod(prog, fused=True)
    ph_cc(prog, "AllGather", [list(range(8))],
          prog.D("modp", [2, NB, 1536], F32).rearrange("l b i -> (l b) i"),
          prog.D("modag", [8, 2, NB, 1536], F32).rearrange("k l b i -> (k l b) i"))
    ph_modcopy(prog)
    ph_pre(prog)
    ph_rope(prog)
    for l in range(2):
        hsrc = prog.D("hTn%d" % l, [16, 128, TOK], BF16).rearrange("j p t -> (j p) t")
        hdst = prog.D("hTfull%d" % l, [4, 2, 4, 128, TOK], BF16)
        for ck in range(4):
            ph_cc(prog, "AllGather", PAIRS, hsrc[ck * 512:(ck + 1) * 512, :],
                  hdst[ck].rearrange("r j p t -> (r j p) t"))
        ph_a1(prog, l)
        ph_a2(prog, l)
        msrc = prog.D("MTh", [2, 1024, TOK], BF16).rearrange("h m t -> (h m) t")
        mdst = prog.D("MTag%d" % l, [4, 2, 512, TOK], BF16)
        for ck in range(4):
            ph_cc(prog, "AllGather", PAIRS, msrc[ck * 512:(ck + 1) * 512, :],
                  mdst[ck].rearrange("s m t -> (s m) t"))
        ph_mtsel(prog, l)
        ph_b3(prog, l)
        ph_b4(prog, l)
        ph_b5(prog, l)
    return prog, ins


def _prep(x, c, positions, w_ada, b_ada, w_in, b_f, lambda_q1, lambda_k1, lambda_q2, lambda_k2, subln_g,
          w_o, ln1_g, ln1_b, w_up, w_down, ln2_g, ln2_b):
    f32 = np.float32
    x = np.asarray(x, f32)
    w_cores = [np.ascontiguousarray(np.asarray(w_in, f32)[:, :, _core_cols(r)]) for r in range(2)]
    rc = _rconst()
    base = []
    for cid in range(8):
        b, r = cid // 2, cid % 2
        base.append({
            "c": np.asarray(c, f32),
            "w_ada_s": np.asarray(w_ada, f32)[:, :, cid * 1536:(cid + 1) * 1536],
            "b_ada_s": np.asarray(b_ada, f32)[:, cid * 1536:(cid + 1) * 1536],
            "x_own": x[b, r * TOK:(r + 1) * TOK],
            "pos_b": np.asarray(positions, np.int32)[b],
            "rconst": rc,
            "w_core": w_cores[r],
            "b_f_s": np.asarray(b_f, f32)[:, 4 * r:4 * r + 4],
            "lambda_q1": np.asarray(lambda_q1, f32), "lambda_k1": np.asarray(lambda_k1, f32),
            "lambda_q2": np.asarray(lambda_q2, f32), "lambda_k2": np.asarray(lambda_k2, f32),
            "subln_g": np.asarray(subln_g, f32), "w_o": np.asarray(w_o, f32),
            "ln1_g": np.asarray(ln1_g, f32), "ln1_b": np.asarray(ln1_b, f32),
            "w_up": np.asarray(w_up, f32), "w_down": np.asarray(w_down, f32),
            "ln2_g": np.asarray(ln2_g, f32), "ln2_b": np.asarray(ln2_b, f32),
        })
    return base


def kernel_fused(**inputs):
    base = _prep(**inputs)
    prog, ins = build_fused()
    maps = [{k: np.ascontiguousarray(m[k]) for k in ins} for m in base]
    res = run_bass_kernel_spmd(prog.nc, maps, core_ids=list(range(8))).results
    out = np.stack([np.concatenate([res[2 * b]["out"], res[2 * b + 1]["out"]], axis=0) for b in range(4)])
    return out.astype(np.float32)


def kernel(**inputs):
    return kernel_unfused(**inputs)
```

```python
import contextlib
import math
import numpy as np
import ml_dtypes
import concourse.bass as bass
import concourse.mybir as mybir
from concourse.bass_utils import run_bass_kernel_spmd

F32 = mybir.dt.float32
BF16 = mybir.dt.bfloat16
I32 = mybir.dt.int32
AF = mybir.ActivationFunctionType
ALU = mybir.AluOpType

D = 2048
SEQ = 4096
NB = 4
DFF = 8192
ALPHA = (2.0 * 2) ** 0.25
LN_EPS = 1e-5
RMS_EPS = 1e-5
FOX_SCALE = 128 ** -0.5
DIFF_SCALE = 64 ** -0.5
NCOL = 32 * 128 + 4
TOK = 2048
TWO_PI = 2.0 * math.pi

ENGS = ("sp", "act", "dve", "pool", "pe")
NDMASEM = 8


def lam_init(l):
    return 0.8 - 0.6 * math.exp(-0.3 * l)


class Sched:
    _uid = 0

    def __init__(self, nc):
        self.nc = nc
        self.ops = []
        self.last_w = {}
        self.readers = {}

    def add(self, eng, fn, reads=(), writes=(), dma=False):
        i = len(self.ops)
        deps = set()
        for r in reads:
            if r in self.last_w:
                deps.add(self.last_w[r])
        for w in writes:
            if w in self.last_w:
                deps.add(self.last_w[w])
            deps.update(self.readers.get(w, ()))
        for r in reads:
            self.readers.setdefault(r, []).append(i)
        for w in writes:
            self.last_w[w] = i
            self.readers[w] = []
        deps.discard(i)
        self.ops.append(dict(eng=eng, fn=fn, deps=deps, dma=dma))
        return i

    def dma(self, q, out, in_, reads=(), writes=()):
        return self.add(q, lambda e: e.dma_start(out=out, in_=in_), reads, writes, dma=True)

    def emit(self):
        nc = self.nc
        ops = self.ops
        for o in ops:
            if o["eng"] == "pe":
                o["deps"] = {d for d in o["deps"] if ops[d]["eng"] != "pe" or ops[d]["dma"]}
        needed = set()
        for o in ops:
            needed.update(o["deps"])
        cnt = {e: 0 for e in ENGS}
        dcnt = {e: 0 for e in ENGS}
        sems = {}
        allsems = []

        def newsem(name):
            h = nc.alloc_semaphore(name=name)
            allsems.append(h)
            return h

        if True:
            for e in ENGS:
                Sched._uid += 1
                sems[e] = newsem("s%d_c_%s" % (Sched._uid, e))
                sems["d_" + e] = [newsem("s%d_d_%s%d" % (Sched._uid, e, k)) for k in range(NDMASEM)]
            for i, o in enumerate(ops):
                e = o["eng"]
                if o["dma"]:
                    k = dcnt[e]
                    dcnt[e] += 1
                    sem = sems["d_" + e][k % NDMASEM]
                    o["sig"] = (sem, 16 * (k // NDMASEM + 1))
                    o["pre"] = (sem, 16 * (k // NDMASEM)) if k >= NDMASEM else None
                    o["do_sig"] = True
                else:
                    o["pre"] = None
                    o["do_sig"] = i in needed
                    if o["do_sig"]:
                        cnt[e] += 1
                        o["sig"] = (sems[e], cnt[e])
            per_eng = {e: [i for i, o in enumerate(ops) if o["eng"] == e] for e in ENGS}
            engobj = {"sp": "sync", "act": "scalar", "dve": "vector", "pool": "gpsimd", "pe": "tensor"}
            with nc.Block() as block:
                def make(ename):
                    def body(eng):
                        waited = {}

                        def wait(sem, val):
                            if val <= 0 or waited.get(id(sem), 0) >= val:
                                return
                            waited[id(sem)] = val
                            eng.wait_ge(sem, val)

                        for i in per_eng[ename]:
                            o = ops[i]
                            if o["pre"] is not None:
                                wait(*o["pre"])
                            for d in sorted(o["deps"]):
                                wait(*ops[d]["sig"])
                            ins = o["fn"](eng)
                            if o["do_sig"]:
                                ins.then_inc(o["sig"][0], 16 if o["dma"] else 1)
                        if ename == "sp":
                            for q in ENGS:
                                n = dcnt[q]
                                for k in range(min(n, NDMASEM)):
                                    uses = (n - 1 - k) // NDMASEM + 1
                                    eng.wait_ge(sems["d_" + q][k], 16 * uses)
                    return body

                for ename in ENGS:
                    getattr(block, engobj[ename])(make(ename))
            nc.clear_and_free_semaphores(allsems)
            nc.all_engine_barrier()


class Prog:
    def __init__(self, kinds, fused=False):
        self.nc = bass.Bass("TRN2", target_bir_lowering=False)
        self.kinds = kinds
        self.dram = {}
        self.fused = fused
        self.regs = {}

    def core_vals(self, e):
        if "v" not in self.regs:
            pid = e.partition_id()
            r = e.snap(pid % 2)
            b = e.snap((pid - r) // 2)
            self.regs["v"] = (r, b)
        return self.regs["v"]

    def D(self, name, shape, dtype):
        if name not in self.dram:
            kind = self.kinds.get(name, "Internal")
            self.dram[name] = self.nc.dram_tensor(name, list(shape), dtype, kind=kind).ap()
        return self.dram[name]


class Phase:
    def __init__(self, prog, tag):
        self.prog = prog
        self.nc = prog.nc
        self.tag = tag
        self.st = contextlib.ExitStack()
        self.S = Sched(self.nc)

    def __enter__(self):
        self.st.__enter__()
        self.banks = [self.st.enter_context(self.nc.psum_tensor("%s_bank%d" % (self.tag, i), [128, 512], F32))
                      for i in range(8)]
        return self

    def __exit__(self, *a):
        if a[0] is None:
            self.S.emit()
        return self.st.__exit__(*a)

    def sb(self, name, shape, dtype):
        return self.st.enter_context(self.nc.sbuf_tensor("%s_%s" % (self.tag, name), list(shape), dtype))

    def ident(self):
        S = self.S
        ones = self.sb("c_ones", [128, 128], F32)
        ident = self.sb("c_ident", [128, 128], F32)
        S.add("pool", lambda e: e.memset(ones[:], 1.0), writes=["c_ones"])
        S.add("pool", lambda e: e.affine_select(out=ident[:], in_=ones[:], pattern=[[-1, 128]], base=0,
                                               channel_multiplier=1, compare_op=ALU.is_equal, fill=0.0),
              reads=["c_ones"], writes=["ident"])
        return ident


class Rot:
    def __init__(self, ph, name, n, shape, dtype):
        self.tiles = [ph.sb("%s%d" % (name, i), shape, dtype) for i in range(n)]
        self.names = ["%s%d" % (name, i) for i in range(n)]
        self.i = 0

    def next(self):
        k = self.i % len(self.tiles)
        self.i += 1
        return self.tiles[k], self.names[k]


def ph_mod(prog, fused=False):
    c = prog.D("c", [NB, D], F32)
    w = prog.D("w_ada_s", [2, D, 1536], F32)
    b = prog.D("b_ada_s", [2, 1536], F32)
    modp = prog.D("modp", [2, NB, 1536], F32)
    with Phase(prog, "mod") as ph:
        S = ph.S
        ident = ph.ident()
        cin = ph.sb("cin", [64, 128], F32)
        cT = ph.sb("cT", [128, 64], F32)
        S.dma("sp", cin[:], c.rearrange("b (j p) -> (b j) p", p=128), writes=["cin"])
        bk = ph.banks[0]
        S.add("pe", lambda e: e.transpose(bk[:, 0:64], cin[:], ident[0:64, 0:64]),
              reads=["cin", "ident"], writes=["bank0"])
        S.add("act", lambda e: e.activation(out=cT[:], in_=bk[:, 0:64], func=AF.Silu),
              reads=["bank0"], writes=["cT"])
        wt = Rot(ph, "wt", 2, [128, 16, 512], F32)
        bt = Rot(ph, "bt", 2, [NB, 512], F32)
        ot = Rot(ph, "ot", 2, [NB, 512], F32)
        k = 0
        for l in range(2):
            for cg in range(3):
                wtile, wn = wt.next()
                btile, bn = bt.next()
                otile, on = ot.next()
                cs = slice(cg * 512, (cg + 1) * 512)
                S.dma("sp", wtile[:], w[l, :, cs].rearrange("(j p) c -> p j c", p=128), writes=[wn])
                S.dma("sp", btile[:], b[l, cs].partition_broadcast(NB), writes=[bn])
                bank = ph.banks[1 + k % 2]
                bname = "bank%d" % (1 + k % 2)
                for j in range(16):
                    S.add("pe", lambda e, j=j, bank=bank, wtile=wtile: e.matmul(
                        bank[0:NB, :], cT[:, j:64:16], wtile[:, j, :], start=(j == 0), stop=(j == 15)),
                        reads=["cT", wn], writes=[bname])
                S.add("dve", lambda e, bank=bank, otile=otile, btile=btile: e.tensor_tensor(
                    out=otile[:], in0=bank[0:NB, :], in1=btile[:], op=ALU.add),
                    reads=[bname, bn], writes=[on])
                S.dma("sp", modp[l, :, cs], otile[:], reads=[on], writes=["modp"])
                k += 1


def _mm(S, out, lhsT, rhs, start, stop, reads, writes):
    S.add("pe", lambda e: e.matmul(out, lhsT, rhs, start=start, stop=stop), reads, writes)


def _tr(S, out, in_, ident, reads, writes):
    S.add("pe", lambda e: e.transpose(out, in_, ident), reads, writes)


def _act(S, out, in_, func, reads, writes, scale=None, bias=None):
    kw = {}
    if scale is not None:
        kw["scale"] = scale
    if bias is not None:
        kw["bias"] = bias
    S.add("act", lambda e: e.activation(out=out, in_=in_, func=func, **kw), reads, writes)


def _tt(S, eng, out, in0, in1, op, reads, writes):
    S.add(eng, lambda e: e.tensor_tensor(out=out, in0=in0, in1=in1, op=op), reads, writes)


def _ts(S, eng, out, in0, s1, op0, reads, writes, s2=None, op1=None):
    if op1 is None:
        S.add(eng, lambda e: e.tensor_scalar(out=out, in0=in0, scalar1=s1, scalar2=None, op0=op0), reads, writes)
    else:
        S.add(eng, lambda e: e.tensor_scalar(out=out, in0=in0, scalar1=s1, scalar2=s2, op0=op0, op1=op1),
              reads, writes)


def _stt(S, out, in0, scalar, in1, op0, op1, reads, writes):
    S.add("dve", lambda e: e.scalar_tensor_tensor(out=out, in0=in0, scalar=scalar, in1=in1, op0=op0, op1=op1),
          reads, writes)


def _copy(S, eng, out, in_, reads, writes):
    S.add(eng, lambda e: e.tensor_copy(out, in_), reads, writes)


def _memset(S, eng, ap, val, writes):
    S.add(eng, lambda e: e.memset(ap, val), (), writes)


def load_modT(ph, ident, modb, l):
    S = ph.S
    rows = ph.sb("modrows", [64, 128], F32)
    modT = ph.sb("modT", [128, 64], F32)
    S.dma("sp", rows[0:32, :], modb[l, 0:2 * D].rearrange("(r p) -> r p", p=128), writes=["modrows"])
    S.dma("sp", rows[32:64, :], modb[l, 3 * D:5 * D].rearrange("(r p) -> r p", p=128), writes=["modrows"])
    bk = ph.banks[7]
    _tr(S, bk[:, 0:64], rows[:], ident[0:64, 0:64], ["modrows", "ident"], ["bank7"])
    _copy(S, "dve", modT[:], bk[:, 0:64], ["bank7"], ["modT"])
    _ts(S, "dve", modT[:, 16:32], modT[:, 16:32], 1.0, ALU.add, ["modT"], ["modT"])
    _ts(S, "dve", modT[:, 48:64], modT[:, 48:64], 1.0, ALU.add, ["modT"], ["modT"])
    return modT


def emit_hT(ph, ident, xt, xn, modT, sh_off, sc_off, hst, hsn, col0, banks):
    S = ph.S
    for jg in range(4):
        bi = banks[jg % len(banks)]
        bank, bn = ph.banks[bi], "bank%d" % bi
        for q in range(4):
            j = jg * 4 + q
            _tr(S, bank[:, q * 128:(q + 1) * 128], xt[:, j * 128:(j + 1) * 128], ident[:], [xn, "ident"], [bn])
        for q in range(4):
            j = jg * 4 + q
            _act(S, hst[:, j, col0:col0 + 128], bank[:, q * 128:(q + 1) * 128], AF.Identity,
                 [bn, "modT"], [hsn + ".%d" % j],
                 scale=modT[:, sc_off + j:sc_off + j + 1], bias=modT[:, sh_off + j:sh_off + j + 1])


def ph_pre(prog):
    x = prog.D("x_own", [TOK, D], F32)
    modb = prog.D("modb", [2, 6 * D], F32)
    hTn = prog.D("hTn0", [16, 128, TOK], BF16)
    with Phase(prog, "pre") as ph:
        S = ph.S
        ident = ph.ident()
        modT = load_modT(ph, ident, modb, 0)
        xp = Rot(ph, "x", 3, [128, D], F32)
        hp = Rot(ph, "hst", 2, [128, 16, 512], BF16)
        for g in range(4):
            hst, hsn = hp.next()
            for q in range(4):
                ts_ = g * 4 + q
                xt, xn = xp.next()
                S.dma("sp", xt[:], x[ts_ * 128:(ts_ + 1) * 128, :], writes=[xn])
                emit_hT(ph, ident, xt, xn, modT, 0, 16, hst, hsn, q * 128, [0, 1, 2, 3])
            S.dma("sp", hTn[:, :, g * 512:(g + 1) * 512].rearrange("j p t -> p j t"), hst[:],
                  reads=[hsn + ".%d" % j for j in range(16)], writes=["hTn"])


MAGIC = 12582912.0


def ph_rope(prog):
    pos = prog.D("pos_b", [SEQ], I32)
    rconst = prog.D("rconst", [128, 2], F32)
    Ctab = prog.D("Ctab", [128, SEQ], F32)
    Stab = prog.D("Stab", [128, SEQ], F32)
    with Phase(prog, "rope") as ph:
        S = ph.S
        rc = ph.sb("rc", [128, 2], F32)
        S.dma("sp", rc[:], rconst, writes=["rc"])
        W = 1024
        pp = Rot(ph, "posi", 2, [128, W], I32)
        ap_ = Rot(ph, "ang", 2, [128, W], F32)
        yp = Rot(ph, "y", 2, [128, W], F32)
        rp = Rot(ph, "r", 2, [128, W], F32)
        op_ = Rot(ph, "o", 2, [128, W], F32)
        for ck in range(SEQ // W):
            posi, pn = pp.next()
            ang, an = ap_.next()
            S.dma("sp", posi[:], pos[ck * W:(ck + 1) * W].partition_broadcast(128), writes=[pn])
            _copy(S, "dve", ang[:], posi[:], [pn], [an])
            _ts(S, "dve", ang[:], ang[:], rc[:, 0:1], ALU.mult, [an, "rc"], [an])
            for which in range(2):
                y, yn = yp.next()
                r, rn = rp.next()
                o, on = op_.next()
                if which == 1:
                    _ts(S, "dve", r[:], ang[:], math.pi / 2, ALU.add, [an], [rn])
                    src, sn = r, rn
                else:
                    src, sn = ang, an
                _ts(S, "dve", y[:], src[:], 1.0 / TWO_PI, ALU.mult, [sn], [yn])
                _ts(S, "dve", y[:], y[:], MAGIC, ALU.add, [yn], [yn])
                _ts(S, "dve", y[:], y[:], MAGIC, ALU.subtract, [yn], [yn])
                _stt(S, r[:], y[:], -TWO_PI, src[:], ALU.mult, ALU.add, [yn, sn], [rn])
                _ts(S, "dve", r[:], r[:], -math.pi, ALU.max, [rn], [rn], s2=math.pi, op1=ALU.min)
                _act(S, o[:], r[:], AF.Sin, [rn], [on])
                if which == 0:
                    _ts(S, "dve", o[:], o[:], rc[:, 1:2], ALU.mult, [on, "rc"], [on], s2=DIFF_SCALE ** 0.5, op1=ALU.mult)
                else:
                    _ts(S, "dve", o[:], o[:], DIFF_SCALE ** 0.5, ALU.mult, [on], [on])
                S.dma("sp", (Stab if which == 0 else Ctab)[:, ck * W:(ck + 1) * W], o[:], reads=[on],
                      writes=["tab%d" % which])


def ph_a1(prog, l):
    hTfull = prog.D("hTfull%d" % l, [4, 2, 4, 128, TOK], BF16)
    wc = prog.D("w_core", [2, D, NCOL], F32)
    bfs = prog.D("b_f_s", [2, 4], F32)
    Ctab = prog.D("Ctab", [128, SEQ], F32)
    Stab = prog.D("Stab", [128, SEQ], F32)
    QT = prog.D("QT", [8, 128, SEQ], BF16)
    KT = prog.D("KT", [8, 128, SEQ], BF16)
    V = prog.D("V", [SEQ, 1024], BF16)
    QA = prog.D("QA", [4, 6, SEQ], BF16)
    KA = prog.D("KA", [4, 6, SEQ], BF16)
    with Phase(prog, "a1_%d" % l) as ph:
        S = ph.S
        hT = ph.sb("hT", [128, 16, SEQ], BF16)
        for rr in range(2):
            for jg in range(4):
                S.dma("sp", hT[:, jg * 4:(jg + 1) * 4, rr * TOK:(rr + 1) * TOK],
                      hTfull[jg, rr].rearrange("j p t -> p j t"), writes=["hT.%d.%d" % (rr, jg)])

        def hres(tok0, j):
            return "hT.%d.%d" % (tok0 // TOK, j // 4)

        nb = [0]

        def nextbank():
            b = nb[0] % 6
            nb[0] += 1
            return ph.banks[b], "bank%d" % b

        wp = Rot(ph, "wp", 4, [128, 16, 128], BF16)
        ost = Rot(ph, "ost", 2, [128, 2048], BF16)

        def load_wpiece(ci):
            t, n = wp.next()
            S.dma("pool", t[:], wc[l, :, ci * 128:(ci + 1) * 128].rearrange("(j p) c -> p j c", p=128), writes=[n])
            return t, n

        for ci in range(8):
            hl, isq = ci % 4, ci < 4
            wt_, wn = load_wpiece(ci)
            for half in range(2):
                ot, on = ost.next()
                for t4 in range(4):
                    tt = half * 4 + t4
                    bank, bn = nextbank()
                    for j in range(16):
                        _mm(S, bank[:], wt_[:, j, :], hT[:, j, tt * 512:(tt + 1) * 512], j == 0, j == 15,
                            [wn, hres(tt * 512, j)], [bn])
                    _act(S, ot[:, t4 * 512:(t4 + 1) * 512], bank[:], AF.Copy, [bn], [on + ".%d" % t4],
                         scale=(FOX_SCALE if isq else 1.0))
                S.dma("sp", (QT if isq else KT)[hl, :, half * 2048:(half + 1) * 2048], ot[:],
                      reads=[on + ".%d" % t4 for t4 in range(4)], writes=["QKd"])
        ctp = Rot(ph, "ct", 2, [128, 512], F32)
        stp = Rot(ph, "st", 2, [128, 512], F32)
        t1p = Rot(ph, "t1", 1, [128, 512], F32)
        t2p = Rot(ph, "t2", 1, [128, 512], F32)
        for ci in range(8, 16):
            hl, isq = 4 + ci % 4, ci < 12
            wm, wmn = load_wpiece(ci)
            wr, wrn = load_wpiece(ci + 8)
            for half in range(2):
                ot, on = ost.next()
                for t4 in range(4):
                    tt = half * 4 + t4
                    ct, cn = ctp.next()
                    st_, sn = stp.next()
                    S.dma("sp", ct[:], Ctab[:, tt * 512:(tt + 1) * 512], writes=[cn])
                    S.dma("sp", st_[:], Stab[:, tt * 512:(tt + 1) * 512], writes=[sn])
                    bm, bmn = nextbank()
                    br, brn = nextbank()
                    for j in range(16):
                        _mm(S, bm[:], wm[:, j, :], hT[:, j, tt * 512:(tt + 1) * 512], j == 0, j == 15,
                            [wmn, hres(tt * 512, j)], [bmn])
                    for j in range(16):
                        _mm(S, br[:], wr[:, j, :], hT[:, j, tt * 512:(tt + 1) * 512], j == 0, j == 15,
                            [wrn, hres(tt * 512, j)], [brn])
                    t1, t1n = t1p.next()
                    t2, t2n = t2p.next()
                    _tt(S, "dve", t1[:], bm[:], ct[:], ALU.mult, [bmn, cn], [t1n])
                    _tt(S, "dve", t2[:], br[:], st_[:], ALU.mult, [brn, sn], [t2n])
                    _tt(S, "pool", ot[:, t4 * 512:(t4 + 1) * 512], t1[:], t2[:], ALU.add, [t1n, t2n],
                        [on + ".%d" % t4])
                S.dma("sp", (QT if isq else KT)[hl, :, half * 2048:(half + 1) * 2048], ot[:],
                      reads=[on + ".%d" % t4 for t4 in range(4)], writes=["QKd"])
        wv = ph.sb("wv", [128, 16, 512], BF16)
        vsp = Rot(ph, "vs", 2, [128, 512], BF16)
        for g in range(2):
            S.dma("pool", wv[:], wc[l, :, 3072 + g * 512:3072 + (g + 1) * 512].rearrange("(j p) c -> p j c", p=128),
                  writes=["wv"])
            for tk in range(32):
                bank, bn = nextbank()
                for j in range(16):
                    _mm(S, bank[:], hT[:, j, tk * 128:(tk + 1) * 128], wv[:, j, :], j == 0, j == 15,
                        ["wv", hres(tk * 128, j)], [bn])
                vs, vn = vsp.next()
                _act(S, vs[:], bank[:], AF.Copy, [bn], [vn])
                S.dma("sp", V[tk * 128:(tk + 1) * 128, g * 512:(g + 1) * 512], vs[:], reads=[vn], writes=["Vd"])
        wg = ph.sb("wg", [128, 16, 4], BF16)
        S.dma("pool", wg[:], wc[l, :, 4096:4100].rearrange("(j p) c -> p j c", p=128), writes=["wg"])
        bfc = ph.sb("bfc", [4, 1], F32)
        S.dma("sp", bfc[:], bfs[l].rearrange("(p o) -> p o", o=1), writes=["bfc"])
        _ts(S, "dve", bfc[:], bfc[:], -1.0, ALU.mult, ["bfc"], ["bfc"])
        one4 = ph.sb("one4", [4, 512], F32)
        _memset(S, "dve", one4[:], 1.0, ["one4"])
        oneb = ph.sb("oneb", [4, 3, 512], BF16)
        _memset(S, "dve", oneb[:], 1.0, ["oneb"])
        ep = Rot(ph, "ge", 1, [4, 512], F32)
        gp = Rot(ph, "gG", 2, [4, 512], F32)
        r1p = Rot(ph, "gr1", 1, [4, 512], F32)
        pcp = Rot(ph, "gpc", 1, [4, 3, 512], BF16)
        ngp = Rot(ph, "gng", 1, [4, 3, 512], BF16)
        prevG = None
        for tt in range(8):
            bank, bn = nextbank()
            for j in range(16):
                _mm(S, bank[0:4, :], wg[:, j, :], hT[:, j, tt * 512:(tt + 1) * 512], j == 0, j == 15,
                    ["wg", hres(tt * 512, j)], [bn])
            e_, en = ep.next()
            _act(S, e_[:], bank[0:4, :], AF.Exp, [bn, "bfc"], [en], scale=-1.0, bias=bfc[:, 0:1])
            _act(S, e_[:], e_[:], AF.Ln, [en], [en], bias=1.0)
            G, gn = gp.next()
            if prevG is None:
                S.add("dve", lambda e, G=G, e_=e_: e.tensor_tensor_scan(
                    out=G[:], data0=one4[:], data1=e_[:], initial=0.0, op0=ALU.mult, op1=ALU.add),
                    ["one4", en], [gn])
            else:
                pG, pgn = prevG
                S.add("dve", lambda e, G=G, e_=e_, pG=pG: e.tensor_tensor_scan(
                    out=G[:], data0=one4[:], data1=e_[:], initial=pG[:, 511:512], op0=ALU.mult, op1=ALU.add),
                    ["one4", en, pgn], [gn])
            prevG = (G, gn)
            pc, pcn = pcp.next()
            ng, ngn = ngp.next()
            r1, r1n = r1p.next()
            _copy(S, "dve", pc[:, 0, :], G[:], [gn], [pcn])
            _tt(S, "dve", r1[:], G[:], pc[:, 0, :], ALU.subtract, [gn, pcn], [r1n])
            _copy(S, "dve", pc[:, 1, :], r1[:], [r1n], [pcn])
            _tt(S, "dve", r1[:], r1[:], pc[:, 1, :], ALU.subtract, [r1n, pcn], [r1n])
            _copy(S, "dve", pc[:, 2, :], r1[:], [r1n], [pcn])
            _ts(S, "dve", ng[:], pc[:], -1.0, ALU.mult, [pcn], [ngn])
            cs = slice(tt * 512, (tt + 1) * 512)
            S.dma("sp", KA[:, 3:6, cs], pc[:], reads=[pcn], writes=["KAd"])
            S.dma("sp", QA[:, 0:3, cs], ng[:], reads=[ngn], writes=["QAd"])
            S.dma("sp", KA[:, 0:3, cs], oneb[:], reads=["oneb"], writes=["KAd"])
            S.dma("sp", QA[:, 3:6, cs], oneb[:], reads=["oneb"], writes=["QAd"])


def ph_a2(prog, l):
    QT = prog.D("QT", [8, 128, SEQ], BF16)
    KT = prog.D("KT", [8, 128, SEQ], BF16)
    V = prog.D("V", [SEQ, 1024], BF16)
    QA = prog.D("QA", [4, 6, SEQ], BF16)
    KA = prog.D("KA", [4, 6, SEQ], BF16)
    lams = [prog.D(n, [2, 64], F32) for n in ("lambda_q1", "lambda_k1", "lambda_q2", "lambda_k2")]
    subg = prog.D("subln_g", [2, 128], F32)
    MT = prog.D("MTh", [2, 1024, TOK], BF16)
    with Phase(prog, "a2_%d" % l) as ph:
        S = ph.S
        onesb = ph.sb("onesb", [128, 128], BF16)
        tri = ph.sb("tri", [128, 128], F32)
        zf = ph.sb("zf", [128, 128], F32)
        onesf = ph.sb("onesf", [128, 128], F32)
        _memset(S, "pool", onesb[:], 1.0, ["onesb"])
        _memset(S, "pool", onesf[:], 1.0 / 128.0, ["onesf"])
        _memset(S, "pool", zf[:], 0.0, ["zf"])
        S.add("pool", lambda e: e.affine_select(out=tri[:], in_=zf[:], pattern=[[1, 128]], base=0,
                                               channel_multiplier=-1, compare_op=ALU.is_ge, fill=-30000.0),
              ["zf"], ["tri"])
        lt = [ph.sb("lam%d" % i, [128, 64], F32) for i in range(4)]
        for i in range(4):
            S.dma("sp", lt[i][:], lams[i][l].partition_broadcast(128), writes=["lam%d" % i])
        lsum = ph.sb("lsum", [128, 2], F32)
        nlam = ph.sb("nlam", [128, 1], F32)
        for m in range(2):
            _tt(S, "dve", lt[2 * m][:], lt[2 * m][:], lt[2 * m + 1][:], ALU.mult,
                ["lam%d" % (2 * m), "lam%d" % (2 * m + 1)], ["lam%d" % (2 * m)])
            S.add("dve", lambda e, m=m: e.reduce_sum(out=lsum[:, m:m + 1], in_=lt[2 * m][:],
                                                     axis=mybir.AxisListType.X),
                  ["lam%d" % (2 * m)], ["lsum"])
        _act(S, lsum[:], lsum[:], AF.Exp, ["lsum"], ["lsum"])
        _stt(S, nlam[:], lsum[:, 1:2], -lam_init(l), lsum[:, 0:1], ALU.add, ALU.subtract, ["lsum"], ["nlam"])
        gcol = ph.sb("gcol", [128, 1], F32)
        S.dma("sp", gcol[:], subg[l].rearrange("(p o) -> p o", o=1), writes=["gcol"])
        _ts(S, "dve", gcol[:], gcol[:], 1.0 - lam_init(l), ALU.mult, ["gcol"], ["gcol"])

        NSET = 2
        KTt = Rot(ph, "KTt", NSET, [128, SEQ], BF16)
        QTt = Rot(ph, "QTt", NSET, [128, SEQ], BF16)
        Vt = Rot(ph, "Vt", NSET, [128, 32, 128], BF16)
        QAt = Rot(ph, "QAt", NSET, [128, SEQ], BF16)
        KAt = Rot(ph, "KAt", NSET, [128, SEQ], BF16)
        for i in range(NSET):
            _memset(S, "pool", QAt.tiles[i][:], 0.0, [QAt.names[i]])
            _memset(S, "pool", KAt.tiles[i][:], 0.0, [KAt.names[i]])
        mtp = Rot(ph, "mt", 2, [128, SEQ], BF16)
        pTp = Rot(ph, "pT", 4, [128, 512], BF16)
        rvp = Rot(ph, "rinv", 2, [128, 512], F32)
        o1p = Rot(ph, "o1", 1, [128, 512], F32)
        o2p = Rot(ph, "o2", 1, [128, 512], F32)
        dp = Rot(ph, "dd", 1, [128, 512], F32)
        sqp = Rot(ph, "sq", 1, [128, 512], F32)
        rsp = Rot(ph, "rstd", 1, [128, 512], F32)
        ucount = [0]

        for hl in range(8):
            fox = hl < 4
            kt, ktn = KTt.next()
            qt, qtn = QTt.next()
            vt, vtn = Vt.next()
            S.dma("sp", kt[:], KT[hl], writes=[ktn])
            S.dma("sp", qt[:], QT[hl], writes=[qtn])
            for kq in range(4):
                S.dma("sp", vt[:, kq * 8:(kq + 1) * 8, :],
                      V[kq * 1024:(kq + 1) * 1024, hl * 128:(hl + 1) * 128].rearrange("(kb p) d -> p kb d", p=128),
                      writes=[vtn + ".%d" % kq])
            if fox:
                qa, qan = QAt.next()
                ka, kan = KAt.next()
                S.dma("sp", qa[0:6, :], QA[hl], writes=[qan])
                S.dma("sp", ka[0:6, :], KA[hl], writes=[kan])
            mt, mtn = mtp.next()
            maps = [(0, 128)] if fox else [(0, 64), (64, 128)]
            units = []
            for qb in range(8):
                for kb in range(4 * qb + 4):
                    for m in range(len(maps)):
                        units.append((qb, kb, m))
            info = {}

            def emit_S(u):
                qb, kb, m = units[u]
                g = ucount[0] + u
                bi = g % 3
                bank, bn = ph.banks[bi], "bank%d" % bi
                r_ = kb - 4 * qb
                c0 = 128 * r_ if r_ > 0 else 0
                lo, hi = maps[m]
                q0 = qb * 512
                _mm(S, bank[:, c0:512], kt[lo:hi, kb * 128:(kb + 1) * 128], qt[lo:hi, q0 + c0:q0 + 512],
                    True, not fox, [ktn, qtn], [bn])
                if fox:
                    _mm(S, bank[:, c0:512], ka[:, kb * 128:(kb + 1) * 128], qa[:, q0 + c0:q0 + 512],
                        False, True, [kan, qan], [bn])
                info[u] = (bank, bn, c0, r_)

            def emit_rest(u):
                qb, kb, m = units[u]
                bank, bn, c0, r_ = info.pop(u)
                pT, pn = pTp.next()
                if r_ >= 0:
                    _tt(S, "dve", bank[:, c0:c0 + 128], bank[:, c0:c0 + 128], tri[:], ALU.add, [bn, "tri"], [bn])
                _act(S, pT[:, c0:512], bank[:, c0:512], AF.Exp, [bn], [pn])
                pair = (qb % 2) if fox else m
                bo, bon = ph.banks[4 + 2 * pair], "bank%d" % (4 + 2 * pair)
                brr, brn = ph.banks[5 + 2 * pair], "bank%d" % (5 + 2 * pair)
                last = kb == 4 * qb + 3
                _mm(S, bo[:, c0:512], vt[:, kb, :], pT[:, c0:512], kb == 0, last, [vtn + ".%d" % (kb // 8), pn], [bon])
                _mm(S, brr[:, c0:512], onesb[:], pT[:, c0:512], kb == 0, last, ["onesb", pn], [brn])
                if last and m == len(maps) - 1:
                    cs = slice(qb * 512, (qb + 1) * 512)
                    if fox:
                        rv, rvn = rvp.next()
                        S.add("dve", lambda e, rv=rv, brr=brr: e.reciprocal(rv[:], brr[:]), [brn], [rvn])
                        _tt(S, "dve", mt[:, cs], bo[:], rv[:], ALU.mult, [bon, rvn], [mtn + ".%d" % qb])
                    else:
                        outs = []
                        for mm_, (op_, nm) in enumerate(((o1p, "o1"), (o2p, "o2"))):
                            bo_, bon_ = ph.banks[4 + 2 * mm_], "bank%d" % (4 + 2 * mm_)
                            br_, brn_ = ph.banks[5 + 2 * mm_], "bank%d" % (5 + 2 * mm_)
                            rv, rvn = rvp.next()
                            S.add("dve", lambda e, rv=rv, br_=br_: e.reciprocal(rv[:], br_[:]), [brn_], [rvn])
                            o_, on_ = op_.next()
                            _tt(S, "dve", o_[:], bo_[:], rv[:], ALU.mult, [bon_, rvn], [on_])
                            outs.append((o_, on_))
                        d_, dn = dp.next()
                        _stt(S, d_[:], outs[1][0][:], nlam[:, 0:1], outs[0][0][:], ALU.mult, ALU.add,
                             [outs[0][1], outs[1][1], "nlam"], [dn])
                        sq, sqn = sqp.next()
                        _tt(S, "pool", sq[:], d_[:], d_[:], ALU.mult, [dn], [sqn])
                        _mm(S, ph.banks[3][:], onesf[:], sq[:], True, True, ["onesf", sqn], ["bank3"])
                        rs, rsn = rsp.next()
                        _act(S, rs[:], ph.banks[3][:], AF.Ln, ["bank3"], [rsn], bias=RMS_EPS)
                        _act(S, rs[:], rs[:], AF.Exp, [rsn], [rsn], scale=-0.5)
                        _stt(S, mt[:, cs], d_[:], gcol[:, 0:1], rs[:], ALU.mult, ALU.mult, [dn, rsn, "gcol"],
                             [mtn + ".%d" % qb])

            n = len(units)
            emit_S(0)
            if n > 1:
                emit_S(1)
            for i in range(n):
                if i + 2 < n:
                    emit_S(i + 2)
                emit_rest(i)
            ucount[0] += n
            for hf in range(2):
                S.dma("sp", MT[hf, hl * 128:(hl + 1) * 128, :], mt[:, hf * TOK:(hf + 1) * TOK],
                      reads=[mtn + ".%d" % qb for qb in range(4 * hf, 4 * hf + 4)], writes=["MTd"])


class LNBufs:
    def __init__(self, ph):
        self.stats = ph.sb("ln_stats", [128, 24], F32)
        self.mv = ph.sb("ln_mv", [128, 2], F32)
        self.rstd = ph.sb("ln_rstd", [128, 1], F32)
        self.nmr = ph.sb("ln_nmr", [128, 1], F32)


def emit_ln(ph, lb, u, un, lng, lnb, xo, xon):
    S = ph.S
    for c in range(4):
        S.add("dve", lambda e, c=c: e.bn_stats(out=lb.stats[:, c * 6:(c + 1) * 6], in_=u[:, c * 512:(c + 1) * 512]),
              [un], ["ln_stats"])
    S.add("dve", lambda e: e.bn_aggr(out=lb.mv[:], in_=lb.stats[:]), ["ln_stats"], ["ln_mv"])
    _act(S, lb.rstd[:], lb.mv[:, 1:2], AF.Ln, ["ln_mv"], ["ln_rstd"], bias=LN_EPS)
    _act(S, lb.rstd[:], lb.rstd[:], AF.Exp, ["ln_rstd"], ["ln_rstd"], scale=-0.5)
    _stt(S, lb.nmr[:], lb.mv[:, 0:1], -1.0, lb.rstd[:], ALU.mult, ALU.mult, ["ln_mv", "ln_rstd"], ["ln_nmr"])
    _act(S, xo[:], u[:], AF.Identity, [un, "ln_rstd", "ln_nmr"], [xon], scale=lb.rstd[:, 0:1], bias=lb.nmr[:, 0:1])
    _tt(S, "pool", xo[:], xo[:], lng[:], ALU.mult, [xon, "lng"], [xon])
    _tt(S, "pool", xo[:], xo[:], lnb[:], ALU.add, [xon, "lnb"], [xon])


def load_bcast(ph, name, src1d, n=D):
    t = ph.sb(name, [128, n], F32)
    ph.S.dma("sp", t[:], src1d.partition_broadcast(128), writes=[name])
    return t


def ph_b3(prog, l):
    MTx = prog.D("MTx%d" % l, [2, 1024, TOK], BF16)
    xres = prog.D("x_own", [TOK, D], F32) if l == 0 else prog.D("x2", [TOK, D], F32)
    wo = prog.D("w_o", [2, D, D], F32)
    modb = prog.D("modb", [2, 6 * D], F32)
    g1 = prog.D("ln1_g", [2, D], F32)
    b1 = prog.D("ln1_b", [2, D], F32)
    x1 = prog.D("x1", [TOK, D], F32)
    h2T = prog.D("h2T", [16, 128, TOK], BF16)
    with Phase(prog, "b3_%d" % l) as ph:
        S = ph.S
        ident = ph.ident()
        modT = load_modT(ph, ident, modb, l)
        G1 = load_bcast(ph, "G1", modb[l, 2 * D:3 * D])
        _ts(S, "pool", G1[:], G1[:], 1.0, ALU.add, ["G1"], ["G1"])
        lng = load_bcast(ph, "lng", g1[l])
        lnb = load_bcast(ph, "lnb", b1[l])
        wot = ph.sb("wo", [128, 16, D], BF16)
        for g4 in range(4):
            S.dma("pool", wot[:, g4 * 4:(g4 + 1) * 4, :],
                  wo[l, g4 * 512:(g4 + 1) * 512, :].rearrange("(g p) c -> p g c", p=128), writes=["wo.%d" % g4])
        lb = LNBufs(ph)
        mtp = Rot(ph, "mtt", 2, [128, 16, 512], BF16)
        xp = Rot(ph, "xt", 2, [128, D], F32)
        up = Rot(ph, "u", 1, [128, D], F32)
        xop = Rot(ph, "xo", 2, [128, D], F32)
        hp = Rot(ph, "hst", 1, [128, 16, 512], BF16)
        for g in range(4):
            mtt, mtn = mtp.next()
            tsl = slice(g * 512, (g + 1) * 512)
            for j in range(2):
                src = MTx[j].rearrange("(h p) t -> p h t", p=128)
                S.dma("sp", mtt[:, 4 * j:4 * j + 4, :], src[:, 0:4, tsl], writes=[mtn + ".%d" % j])
                S.dma("sp", mtt[:, 8 + 4 * j:12 + 4 * j, :], src[:, 4:8, tsl], writes=[mtn + ".%d" % (2 + j)])
            mres = [mtn + ".%d" % i for i in range(4)]
            hst, hsn = hp.next()
            for q in range(4):
                ts_ = g * 4 + q
                xt, xn = xp.next()
                S.dma("sp", xt[:], xres[ts_ * 128:(ts_ + 1) * 128, :], writes=[xn])
                u, un = up.next()
                for cg in range(4):
                    bank, bn = ph.banks[cg], "bank%d" % cg
                    for gg in range(16):
                        _mm(S, bank[:], mtt[:, gg, q * 128:(q + 1) * 128], wot[:, gg, cg * 512:(cg + 1) * 512],
                            gg == 0, gg == 15, mres + ["wo.%d" % (gg // 4)], [bn])
                    _tt(S, "dve", u[:, cg * 512:(cg + 1) * 512], bank[:], G1[:, cg * 512:(cg + 1) * 512], ALU.mult,
                        [bn, "G1"], [un])
                _stt(S, u[:], xt[:], ALPHA, u[:], ALU.mult, ALU.add, [xn, un], [un])
                xo, xon = xop.next()
                emit_ln(ph, lb, u, un, lng, lnb, xo, xon)
                S.dma("sp", x1[ts_ * 128:(ts_ + 1) * 128, :], xo[:], reads=[xon], writes=["x1d"])
                emit_hT(ph, ident, xo, xon, modT, 32, 48, hst, hsn, q * 128, [4, 5, 6])
            S.dma("sp", h2T[:, :, tsl].rearrange("j p t -> p j t"), hst[:],
                  reads=[hsn + ".%d" % j for j in range(16)], writes=["h2Td"])


def ph_b4(prog, l):
    h2T = prog.D("h2T", [16, 128, TOK], BF16)
    wu = prog.D("w_up", [2, D, DFF], F32)
    wd = prog.D("w_down", [2, DFF, D], F32)
    y2 = prog.D("y2", [TOK, D], F32)
    T = 1024
    with Phase(prog, "b4_%d" % l) as ph:
        S = ph.S
        h2t = ph.sb("h2t", [128, 16, T], BF16)
        aT = ph.sb("aT", [128, 64, T], BF16)
        wup = Rot(ph, "wu", 2, [128, 16, 256], BF16)
        wdp = Rot(ph, "wd", 3, [128, 4, 512], BF16)
        rp = Rot(ph, "relu", 2, [128, 512], F32)
        yp = Rot(ph, "ys", 4, [128, 512], F32)
        nbk = [0]
        for tile in range(TOK // T):
            t0 = tile * T
            for jg in range(4):
                S.dma("sp", h2t[:, jg * 4:(jg + 1) * 4, :],
                      h2T[jg * 4:(jg + 1) * 4, :, t0:t0 + T].rearrange("j p t -> p j t"), writes=["h2t.%d" % jg])
            for fc2 in range(32):
                w_, wn = wup.next()
                S.dma("pool", w_[:], wu[l, :, fc2 * 256:(fc2 + 1) * 256].rearrange("(j p) c -> p j c", p=128),
                      writes=[wn])
                for f in range(2):
                    fc = fc2 * 2 + f
                    for th in range(2):
                        bi = nbk[0] % 8
                        nbk[0] += 1
                        bank, bn = ph.banks[bi], "bank%d" % bi
                        for j in range(16):
                            _mm(S, bank[:], w_[:, j, f * 128:(f + 1) * 128], h2t[:, j, th * 512:(th + 1) * 512],
                                j == 0, j == 15, [wn, "h2t.%d" % (j // 4)], [bn])
                        r_, rn = rp.next()
                        _act(S, r_[:], bank[:], AF.Relu, [bn], [rn])
                        _tt(S, "dve", aT[:, fc, th * 512:(th + 1) * 512], r_[:], r_[:], ALU.mult, [rn],
                            ["aT.%d.%d" % (fc, th)])
            for cg in range(4):
                for fg in range(16):
                    w_, wn = wdp.next()
                    S.dma("pool", w_[:], wd[l, fg * 512:(fg + 1) * 512, cg * 512:(cg + 1) * 512].rearrange(
                        "(f p) c -> p f c", p=128), writes=[wn])
                    for f in range(4):
                        fc = fg * 4 + f
                        for ts_ in range(8):
                            _mm(S, ph.banks[ts_][:], aT[:, fc, ts_ * 128:(ts_ + 1) * 128], w_[:, f, :],
                                fc == 0, fc == 63, [wn, "aT.%d.%d" % (fc, ts_ // 4)], ["bank%d" % ts_])
                for ts_ in range(8):
                    ys, yn = yp.next()
                    if ts_ % 2 == 0:
                        _copy(S, "dve", ys[:], ph.banks[ts_][:], ["bank%d" % ts_], [yn])
                    else:
                        _act(S, ys[:], ph.banks[ts_][:], AF.Copy, ["bank%d" % ts_], [yn])
                    S.dma("sp", y2[t0 + ts_ * 128:t0 + (ts_ + 1) * 128, cg * 512:(cg + 1) * 512], ys[:], reads=[yn],
                          writes=["y2d"])


def ph_b5(prog, l):
    y2 = prog.D("y2", [TOK, D], F32)
    x1 = prog.D("x1", [TOK, D], F32)
    modb = prog.D("modb", [2, 6 * D], F32)
    g2 = prog.D("ln2_g", [2, D], F32)
    b2 = prog.D("ln2_b", [2, D], F32)
    last = l == 1
    xout = prog.D("out", [TOK, D], F32) if last else prog.D("x2", [TOK, D], F32)
    hTn = None if last else prog.D("hTn1", [16, 128, TOK], BF16)
    with Phase(prog, "b5_%d" % l) as ph:
        S = ph.S
        ident = ph.ident()
        modT = None if last else load_modT(ph, ident, modb, l + 1)
        G2 = load_bcast(ph, "G2", modb[l, 5 * D:6 * D])
        _ts(S, "pool", G2[:], G2[:], 1.0, ALU.add, ["G2"], ["G2"])
        lng = load_bcast(ph, "lng", g2[l])
        lnb = load_bcast(ph, "lnb", b2[l])
        lb = LNBufs(ph)
        yp = Rot(ph, "yt", 2, [128, D], F32)
        xp = Rot(ph, "xt", 2, [128, D], F32)
        xop = Rot(ph, "xo", 2, [128, D], F32)
        hp = Rot(ph, "hst", 2, [128, 16, 512], BF16)
        for g in range(4):
            if not last:
                hst, hsn = hp.next()
            for q in range(4):
                ts_ = g * 4 + q
                rows = slice(ts_ * 128, (ts_ + 1) * 128)
                yt, yn = yp.next()
                xt, xn = xp.next()
                S.dma("sp", yt[:], y2[rows, :], writes=[yn])
                S.dma("sp", xt[:], x1[rows, :], writes=[xn])
                _tt(S, "dve", yt[:], yt[:], G2[:], ALU.mult, [yn, "G2"], [yn])
                _stt(S, yt[:], xt[:], ALPHA, yt[:], ALU.mult, ALU.add, [xn, yn], [yn])
                xo, xon = xop.next()
                emit_ln(ph, lb, yt, yn, lng, lnb, xo, xon)
                S.dma("sp", xout[rows, :], xo[:], reads=[xon], writes=["xoutd"])
                if not last:
                    emit_hT(ph, ident, xo, xon, modT, 0, 16, hst, hsn, q * 128, [0, 1, 2, 3])
            if not last:
                S.dma("sp", hTn[:, :, g * 512:(g + 1) * 512].rearrange("j p t -> p j t"), hst[:],
                      reads=[hsn + ".%d" % j for j in range(16)], writes=["hTnd"])


def _core_cols(r):
    off = {"fq": 0, "fk": 1024, "fv": 2048, "ff": 3072, "dq": 3080, "dk": 4104, "dv": 5128}
    heads = [4 * r + i for i in range(4)]
    cols = []
    for name in ("fq", "fk", "dq", "dk"):
        for h in heads:
            cols += [off[name] + h * 128 + d for d in range(128)]
    for name in ("dq", "dk"):
        for h in heads:
            for d in range(128):
                i = d % 64
                src = d + 8 if i < 8 else (d - 8 if i < 16 else d)
                cols.append(off[name] + h * 128 + src)
    for name in ("fv", "dv"):
        for h in heads:
            cols += [off[name] + h * 128 + d for d in range(128)]
    cols += [off["ff"] + h for h in heads]
    return np.array(cols)


def _rconst():
    rc = np.zeros((128, 2), np.float32)
    inv = (np.float32(500000.0) ** (-np.arange(0, 16, 2, dtype=np.float32) / np.float32(16))).astype(np.float32)
    for base in (0, 64):
        for i in range(16):
            rc[base + i, 0] = inv[i % 8]
            rc[base + i, 1] = -1.0 if i < 8 else 1.0
    return rc


def _launch(phases, ins, outs, in_maps):
    kinds = {k: "ExternalInput" for k in ins}
    kinds.update({k: "ExternalOutput" for k in outs})
    prog = Prog(kinds)
    for p in phases:
        p(prog)
    used = [k for k in ins if k in prog.dram]
    maps = [{k: np.ascontiguousarray(m[k]) for k in used} for m in in_maps]
    res = run_bass_kernel_spmd(prog.nc, maps, core_ids=list(range(8)))
    return res.results


def kernel_unfused(x, c, positions, w_ada, b_ada, w_in, b_f, lambda_q1, lambda_k1, lambda_q2, lambda_k2, subln_g,
           w_o, ln1_g, ln1_b, w_up, w_down, ln2_g, ln2_b):
    f32 = np.float32
    x = np.asarray(x, f32)
    w_cores = [np.ascontiguousarray(np.asarray(w_in, f32)[:, :, _core_cols(r)]) for r in range(2)]
    rc = _rconst()
    base = []
    for cid in range(8):
        b, r = cid // 2, cid % 2
        base.append({
            "c": np.asarray(c, f32),
            "w_ada_s": np.asarray(w_ada, f32)[:, :, cid * 1536:(cid + 1) * 1536],
            "b_ada_s": np.asarray(b_ada, f32)[:, cid * 1536:(cid + 1) * 1536],
            "x_own": x[b, r * TOK:(r + 1) * TOK],
            "pos_b": np.asarray(positions, np.int32)[b],
            "rconst": rc,
            "w_core": w_cores[r],
            "b_f_s": np.asarray(b_f, f32)[:, 4 * r:4 * r + 4],
            "lambda_q1": np.asarray(lambda_q1, f32), "lambda_k1": np.asarray(lambda_k1, f32),
            "lambda_q2": np.asarray(lambda_q2, f32), "lambda_k2": np.asarray(lambda_k2, f32),
            "subln_g": np.asarray(subln_g, f32), "w_o": np.asarray(w_o, f32),
            "ln1_g": np.asarray(ln1_g, f32), "ln1_b": np.asarray(ln1_b, f32),
            "w_up": np.asarray(w_up, f32), "w_down": np.asarray(w_down, f32),
            "ln2_g": np.asarray(ln2_g, f32), "ln2_b": np.asarray(ln2_b, f32),
        })
    innames = list(base[0].keys())
    r0 = _launch([ph_mod], innames, ["modp"], base)
    mod_all = np.concatenate([r["modp"] for r in r0], axis=2)
    for cid in range(8):
        base[cid]["modb"] = mod_all[:, cid // 2, :]
    r1 = _launch([ph_pre], innames + ["modb"], ["hTn0"], base)
    hT = [r["hTn0"] for r in r1]
    out = None
    for l in range(2):
        for cid in range(8):
            p = cid - cid % 2
            base[cid]["hTfull%d" % l] = np.stack(
                [np.stack([hT[p + rr][jg * 4:(jg + 1) * 4] for rr in range(2)]) for jg in range(4)])
        ra = _launch([ph_rope, lambda pr, l=l: ph_a1(pr, l), lambda pr, l=l: ph_a2(pr, l)],
                     innames + ["modb", "hTfull%d" % l], ["MTh"], base)
        for cid in range(8):
            p, r = cid - cid % 2, cid % 2
            base[cid]["MTx%d" % l] = np.stack([ra[p]["MTh"][r], ra[p + 1]["MTh"][r]])
        extra = ["modb", "MTx%d" % l] + (["x2"] if l == 1 else [])
        outs = ["out"] if l == 1 else ["x2", "hTn1"]
        rb = _launch([lambda pr, l=l: ph_b3(pr, l), lambda pr, l=l: ph_b4(pr, l), lambda pr, l=l: ph_b5(pr, l)],
                     innames + extra, outs, base)
        if l == 0:
            for cid in range(8):
                base[cid]["x2"] = rb[cid]["x2"]
            hT = [r["hTn1"] for r in rb]
        else:
            out = np.stack([np.concatenate([rb[2 * b]["out"], rb[2 * b + 1]["out"]], axis=0) for b in range(4)])
    return out.astype(np.float32)


PAIRS = [[0, 1], [2, 3], [4, 5], [6, 7]]
_cc_n = [0]


def ph_cc(prog, kind, groups, src2d, dst2d):
    nc = prog.nc
    _cc_n[0] += 1
    cc = nc.alloc_semaphore(name="cc_sem%d" % _cc_n[0])
    with nc.Block() as block:
        @block.gpsimd
        def _(g):
            g.collective_compute(kind, ALU.bypass, replica_groups=groups, ins=[src2d.opt()],
                                 outs=[dst2d.opt()]).then_inc(cc)
            g.wait_ge(cc, 1)
    nc.clear_and_free_semaphores([cc])
    nc.all_engine_barrier()


def ph_modcopy(prog):
    nc = prog.nc
    modag = prog.D("modag", [8, 2, NB, 1536], F32)
    modb = prog.D("modb", [2, 6 * D], F32)
    _cc_n[0] += 1
    sm = nc.alloc_semaphore(name="mc_sem%d" % _cc_n[0])
    with nc.Block() as block:
        @block.sync
        def _(e):
            _, bidx = prog.core_vals(e)
            src = modag.rearrange("k l b i -> b k l i")[bass.ts(bidx, 1)][0]
            e.dma_start(out=modb.rearrange("l (k i) -> k l i", i=1536), in_=src).then_inc(sm, 16)
            e.wait_ge(sm, 16)
    nc.clear_and_free_semaphores([sm])
    nc.all_engine_barrier()


def ph_mtsel(prog, l):
    nc = prog.nc
    MTag = prog.D("MTag%d" % l, [4, 2, 512, TOK], BF16)
    MTx = prog.D("MTx%d" % l, [2, 1024, TOK], BF16)
    _cc_n[0] += 1
    sm = nc.alloc_semaphore(name="ms_sem%d" % _cc_n[0])
    with nc.Block() as block:
        @block.sync
        def _(e):
            r, _ = prog.core_vals(e)
            src = MTag.rearrange("(h mh) s m t -> h mh s m t", mh=2)[bass.ts(r, 1)][0]
            for mh in range(2):
                e.dma_start(out=MTx[:, mh * 512:(mh + 1) * 512, :], in_=src[mh]).then_inc(sm, 16)
            e.wait_ge(sm, 32)
    nc.clear_and_free_semaphores([sm])
    nc.all_engine_barrier()


def build_fused():
    ins = ["c", "w_ada_s", "b_ada_s", "x_own", "pos_b", "rconst", "w_core", "b_f_s", "lambda_q1", "lambda_k1",
           "lambda_q2", "lambda_k2", "subln_g", "w_o", "ln1_g", "ln1_b", "w_up", "w_down", "ln2_g", "ln2_b"]
    kinds = {k: "ExternalInput" for k in ins}
    kinds["out"] = "ExternalOutput"
    prog = Prog(kinds, fused=True)
    ph_mod(prog, fused=True)
    ph_cc(prog, "AllGather", [list(range(8))],
          prog.D("modp", [2, NB, 1536], F32).rearrange("l b i -> (l b) i"),
          prog.D("modag", [8, 2, NB, 1536], F32).rearrange("k l b i -> (k l b) i"))
    ph_modcopy(prog)
    ph_pre(prog)
    ph_rope(prog)
    for l in range(2):
        hsrc = prog.D("hTn%d" % l, [16, 128, TOK], BF16).rearrange("j p t -> (j p) t")
        hdst = prog.D("hTfull%d" % l, [4, 2, 4, 128, TOK], BF16)
        for ck in range(4):
            ph_cc(prog, "AllGather", PAIRS, hsrc[ck * 512:(ck + 1) * 512, :],
                  hdst[ck].rearrange("r j p t -> (r j p) t"))
        ph_a1(prog, l)
        ph_a2(prog, l)
        msrc = prog.D("MTh", [2, 1024, TOK], BF16).rearrange("h m t -> (h m) t")
        mdst = prog.D("MTag%d" % l, [4, 2, 512, TOK], BF16)
        for ck in range(4):
            ph_cc(prog, "AllGather", PAIRS, msrc[ck * 512:(ck + 1) * 512, :],
                  mdst[ck].rearrange("s m t -> (s m) t"))
        ph_mtsel(prog, l)
        ph_b3(prog, l)
        ph_b4(prog, l)
        ph_b5(prog, l)
    return prog, ins


def _prep(x, c, positions, w_ada, b_ada, w_in, b_f, lambda_q1, lambda_k1, lambda_q2, lambda_k2, subln_g,
          w_o, ln1_g, ln1_b, w_up, w_down, ln2_g, ln2_b):
    f32 = np.float32
    x = np.asarray(x, f32)
    w_cores = [np.ascontiguousarray(np.asarray(w_in, f32)[:, :, _core_cols(r)]) for r in range(2)]
    rc = _rconst()
    base = []
    for cid in range(8):
        b, r = cid // 2, cid % 2
        base.append({
            "c": np.asarray(c, f32),
            "w_ada_s": np.asarray(w_ada, f32)[:, :, cid * 1536:(cid + 1) * 1536],
            "b_ada_s": np.asarray(b_ada, f32)[:, cid * 1536:(cid + 1) * 1536],
            "x_own": x[b, r * TOK:(r + 1) * TOK],
            "pos_b": np.asarray(positions, np.int32)[b],
            "rconst": rc,
            "w_core": w_cores[r],
            "b_f_s": np.asarray(b_f, f32)[:, 4 * r:4 * r + 4],
            "lambda_q1": np.asarray(lambda_q1, f32), "lambda_k1": np.asarray(lambda_k1, f32),
            "lambda_q2": np.asarray(lambda_q2, f32), "lambda_k2": np.asarray(lambda_k2, f32),
            "subln_g": np.asarray(subln_g, f32), "w_o": np.asarray(w_o, f32),
            "ln1_g": np.asarray(ln1_g, f32), "ln1_b": np.asarray(ln1_b, f32),
            "w_up": np.asarray(w_up, f32), "w_down": np.asarray(w_down, f32),
            "ln2_g": np.asarray(ln2_g, f32), "ln2_b": np.asarray(ln2_b, f32),
        })
    return base


def kernel_fused(**inputs):
    base = _prep(**inputs)
    prog, ins = build_fused()
    maps = [{k: np.ascontiguousarray(m[k]) for k in ins} for m in base]
    res = run_bass_kernel_spmd(prog.nc, maps, core_ids=list(range(8))).results
    out = np.stack([np.concatenate([res[2 * b]["out"], res[2 * b + 1]["out"]], axis=0) for b in range(4)])
    return out.astype(np.float32)


def kernel(**inputs):
    return kernel_unfused(**inputs)
```
